# Optimizing a Trainium2 kernel written in Bass

```python
import jax, jax.numpy as jnp
from jax import lax
import numpy as np

D_MODEL = 1024
BATCH = 4
SEQ = 8192
DEPTH = 4

GRID_W = 64
CTX_LEN = 256
D_FF = 2816
FFN_RESIDUAL = 0.5
N_MOD = 9
ROPE_BASE = 10000.0
EPS = 1e-6

POOL_GROUPS = 4
POOL_CG = 64
POOL_WIDTH = POOL_GROUPS * POOL_CG
POOL_WINDOWS = (2, 4, 8, 16)
RET_HEADS = 6
RET_DK = 64
RET_DV = 128
RET_CHUNK = 128
RET_QK_W = RET_HEADS * RET_DK
RET_V_W = RET_HEADS * RET_DV
AB_SPLITS = (POOL_WIDTH, POOL_WIDTH + RET_QK_W, POOL_WIDTH + 2 * RET_QK_W, POOL_WIDTH + 2 * RET_QK_W + RET_V_W)
AB_IN = POOL_WIDTH + 2 * RET_QK_W + 2 * RET_V_W
AB_OUT = POOL_WIDTH + RET_V_W

MLA_HEADS = 8
MLA_Q_RANK = 384
MLA_KV_RANK = 256
MLA_NOPE = 64
MLA_ROPE = 32
MLA_V = 128
MLA_QK = MLA_NOPE + MLA_ROPE
C_IN = MLA_Q_RANK + MLA_KV_RANK + MLA_ROPE
Q_BLOCK = 128

N_EVEN = (DEPTH + 1) // 2
N_ODD = DEPTH // 2

kernel_name = "hybrid_pool_retention_mla_prefix_dit"

f32 = jnp.float32


def rms_norm(x, g):
    xf = x.astype(f32)
    y = xf * lax.rsqrt(jnp.mean(xf * xf, axis=-1, keepdims=True) + EPS)
    return (y * g.astype(f32)).astype(x.dtype)


def mod_slice(mod, n):
    return mod[..., n, :][..., None, :]


def ada_modulation(cond, w, b):
    m = jax.nn.silu(cond) @ w + b
    return m.reshape(*cond.shape[:-1], N_MOD, D_MODEL)


def modulate(x, g, shift, scale):
    return rms_norm(x, g) * (1.0 + scale) + shift


def swiglu(h, w_in, w_out):
    a, b = jnp.split(h @ w_in, 2, axis=-1)
    return (jax.nn.silu(a) * b) @ w_out


def macaron_half(x, mod, k, g, w_in, w_out):
    shift, scale, gate = mod_slice(mod, 3 * k), mod_slice(mod, 3 * k + 1), mod_slice(mod, 3 * k + 2)
    return x + FFN_RESIDUAL * gate * swiglu(modulate(x, g, shift, scale), w_in, w_out)


def axial_rope_table(row, col, rot_dim):
    n_freq = rot_dim // 4
    inv = ROPE_BASE ** (-jnp.arange(n_freq, dtype=f32) / n_freq)
    ang = jnp.concatenate([row.astype(f32)[:, None] * inv, col.astype(f32)[:, None] * inv], axis=-1)
    return jnp.cos(ang), jnp.sin(ang)


def apply_rope(x, cos, sin):
    xf = x.astype(f32).reshape(*x.shape[:-1], x.shape[-1] // 2, 2)
    x1, x2 = xf[..., 0], xf[..., 1]
    c, s = cos[:, None, :], sin[:, None, :]
    out = jnp.stack([x1 * c - x2 * s, x1 * s + x2 * c], axis=-1).reshape(x.shape)
    return out.astype(x.dtype)


def heads(t, d):
    return t.reshape(t.shape[0], t.shape[1], -1, d)


def centred_pool_minus_self(p, window):
    L = p.shape[1]
    lo = window // 2
    hi = window - 1 - lo
    pf = p.astype(f32)
    cs = jnp.concatenate([jnp.zeros_like(pf[:, :1]), jnp.cumsum(pf, axis=1)], axis=1)
    t = jnp.arange(L)
    start = jnp.maximum(t - lo, 0)
    end = jnp.minimum(t + hi + 1, L)
    total = jnp.take(cs, end, axis=1) - jnp.take(cs, start, axis=1)
    count = (end - start).astype(f32)[None, :, None]
    return (total / count - pf).astype(p.dtype)


def pool_mixer(p, w_pool, p_scale):
    B, L, _ = p.shape
    groups = jnp.split(p, POOL_GROUPS, axis=-1)
    diffs = jnp.stack([centred_pool_minus_self(gp, w) for gp, w in zip(groups, POOL_WINDOWS)], axis=2)
    y = jnp.einsum('blgc,gcd->blgd', diffs, w_pool)
    return y.reshape(B, L, POOL_WIDTH) * p_scale


def retention_chunked(q, k, v, log_decay, state0):
    B, L, H, dk = q.shape
    dv = v.shape[-1]
    n = L // RET_CHUNK
    qc = q.astype(f32).reshape(B, n, RET_CHUNK, H, dk)
    kc = k.astype(f32).reshape(B, n, RET_CHUNK, H, dk)
    vc = v.astype(f32).reshape(B, n, RET_CHUNK, H, dv)
    lg = log_decay.astype(f32)
    i = jnp.arange(RET_CHUNK, dtype=f32)
    rel = i[:, None] - i[None, :]
    dmask = jnp.where(rel[None] >= 0, jnp.exp(lg[:, None, None] * jnp.maximum(rel, 0.0)[None]), 0.0)
    scores = jnp.einsum('bnihd,bnjhd->bnhij', qc, kc) * dmask
    intra = jnp.einsum('bnhij,bnjhe->bnihe', scores, vc)
    zeta = jnp.exp(lg[:, None] * (RET_CHUNK - 1 - i))
    xi = jnp.exp(lg[:, None] * (i + 1.0))
    chunk_kv = jnp.einsum('bnjhd,hj,bnjhe->nbhde', kc, zeta, vc)
    chunk_decay = jnp.exp(lg * RET_CHUNK)[None, :, None, None]

    def step(state, kv):
        return chunk_decay * state + kv, state

    final, prev = lax.scan(step, state0.astype(f32), chunk_kv)
    cross = jnp.einsum('bnihd,hi,nbhde->bnihe', qc, xi, prev)
    out = (intra + cross).reshape(B, L, H, dv)
    return out.astype(v.dtype), final


def retention_final_state(k, v, log_decay):
    L = k.shape[1]
    w = jnp.exp(log_decay.astype(f32)[:, None] * (L - 1 - jnp.arange(L, dtype=f32)))
    return jnp.einsum('blhd,hl,blhe->bhde', k.astype(f32), w, v.astype(f32))


def bidir_retention(q, k, v, log_decay_fb, state_f, state_b):
    out_f, _ = retention_chunked(q, k, v, log_decay_fb[0], state_f)
    out_b, _ = retention_chunked(q[:, ::-1], k[:, ::-1], v[:, ::-1], log_decay_fb[1], state_b)
    return out_f + out_b[:, ::-1]


def retention_head_out(ret, g, norm_g):
    B, L = ret.shape[:2]
    y = rms_norm(ret, norm_g.reshape(RET_HEADS, RET_DV)).reshape(B, L, RET_V_W)
    return y * jax.nn.silu(g)


def split_ab(proj):
    p, q, k, v, g = jnp.split(proj, AB_SPLITS, axis=-1)
    return p, heads(q, RET_DK), heads(k, RET_DK) * (RET_DK ** -0.5), heads(v, RET_DV), g


def pool_retention_mixer(h, hc, rope, w_in, w_pool, p_scale, log_decay_fb, norm_g, w_out, need_ctx_out):
    p, q, k, v, g = split_ab(h @ w_in)
    pc, qc, kc, vc, gc = split_ab(hc @ w_in)
    state_f = retention_final_state(kc, vc, log_decay_fb[0])
    state_b = retention_final_state(kc[:, ::-1], vc[:, ::-1], log_decay_fb[1])
    q = apply_rope(q, *rope)
    k = apply_rope(k, *rope)
    ret = bidir_retention(q, k, v, log_decay_fb, state_f, state_b)
    y = jnp.concatenate([pool_mixer(p, w_pool, p_scale), retention_head_out(ret, g, norm_g)], axis=-1) @ w_out
    yc = None
    if need_ctx_out:
        zeros = jnp.zeros_like(state_f)
        ret_c = bidir_retention(qc, kc, vc, log_decay_fb, zeros, zeros)
        yc = jnp.concatenate([pool_mixer(pc, w_pool, p_scale), retention_head_out(ret_c, gc, norm_g)], axis=-1) @ w_out
    return y, yc


def mla_queries(qa, qa_g, w_qb, qn_g):
    q = heads(rms_norm(qa, qa_g) @ w_qb, MLA_QK)
    return rms_norm(q, qn_g)


def mla_keys_values(kva, kr, kva_g, w_kvb, kn_g):
    B, L = kva.shape[:2]
    kv = heads(rms_norm(kva, kva_g) @ w_kvb, MLA_NOPE + MLA_V)
    k_nope, v = kv[..., :MLA_NOPE], kv[..., MLA_NOPE:]
    k = jnp.concatenate([k_nope, jnp.broadcast_to(kr[:, :, None, :], (B, L, MLA_HEADS, MLA_ROPE))], axis=-1)
    return rms_norm(k, kn_g), v


def rope_tail(t, cos, sin):
    return jnp.concatenate([t[..., :MLA_NOPE], apply_rope(t[..., MLA_NOPE:], cos, sin)], axis=-1)


def attend(q, k, v):
    s = jnp.einsum('bqhd,bkhd->bhqk', q, k, preferred_element_type=f32) * (q.shape[-1] ** -0.5)
    p = jax.nn.softmax(s, axis=-1).astype(v.dtype)
    return jnp.einsum('bhqk,bkhe->bqhe', p, v)


def latent_attention(q, k, v, k_ctx, v_ctx):
    B, L, H, d = q.shape
    k_all = jnp.concatenate([k, k_ctx], axis=1)
    v_all = jnp.concatenate([v, v_ctx], axis=1)
    nb = L // Q_BLOCK
    qb = q.reshape(B, nb, Q_BLOCK, H, d).transpose(1, 0, 2, 3, 4)
    o = lax.map(lambda qi: attend(qi, k_all, v_all), qb)
    return o.transpose(1, 0, 2, 3, 4).reshape(B, L, H * v.shape[-1])


def mla_mixer(h, hc, rope, w_in, qa_g, kva_g, w_qb, w_kvb, qn_g, kn_g, w_out, need_ctx_out):
    splits = (MLA_Q_RANK, MLA_Q_RANK + MLA_KV_RANK)
    qa, kva, kr = jnp.split(h @ w_in, splits, axis=-1)
    qac, kvac, krc = jnp.split(hc @ w_in, splits, axis=-1)
    q = rope_tail(mla_queries(qa, qa_g, w_qb, qn_g), *rope)
    k, v = mla_keys_values(kva, kr, kva_g, w_kvb, kn_g)
    k = rope_tail(k, *rope)
    k_c, v_c = mla_keys_values(kvac, krc, kva_g, w_kvb, kn_g)
    y = latent_attention(q, k, v, k_c, v_c) @ w_out
    yc = None
    if need_ctx_out:
        q_c = mla_queries(qac, qa_g, w_qb, qn_g)
        o_c = attend(q_c, k_c, v_c)
        yc = o_c.reshape(o_c.shape[0], o_c.shape[1], MLA_HEADS * MLA_V) @ w_out
    return y, yc


def setup_inputs(seed: int = 0) -> dict:
    key = jax.random.key(seed)
    ks = jax.random.split(key, 32)

    def nrm(k, shape, fan_in, gain=1.0):
        return gain * jax.random.normal(k, shape, f32) * (fan_in ** -0.5)

    def gain_vec(k, shape):
        return 1.0 + 0.02 * jax.random.normal(k, shape, f32)

    base_ld = jnp.log1p(-(2.0 ** (-5.0 - jnp.arange(RET_HEADS, dtype=f32))))
    ret_log_decay = base_ld[None, None, :] * (1.0 + 0.05 * jax.random.normal(ks[12], (N_EVEN, 2, RET_HEADS), f32))
    return {
        "x": jax.random.normal(ks[0], (BATCH, SEQ, D_MODEL), f32),
        "c": jax.random.normal(ks[1], (BATCH, D_MODEL), f32),
        "ctx": jax.random.normal(ks[2], (BATCH, CTX_LEN, D_MODEL), f32),
        "c_ctx": jax.random.normal(ks[3], (D_MODEL,), f32),
        "ada_w": nrm(ks[4], (DEPTH, D_MODEL, N_MOD * D_MODEL), D_MODEL, 0.5),
        "ada_b": 0.02 * jax.random.normal(ks[5], (DEPTH, N_MOD * D_MODEL), f32),
        "norm_g": gain_vec(ks[6], (DEPTH, 3, D_MODEL)),
        "ffn_w_in": nrm(ks[7], (DEPTH, 2, D_MODEL, 2 * D_FF), D_MODEL),
        "ffn_w_out": nrm(ks[8], (DEPTH, 2, D_FF, D_MODEL), D_FF),
        "ab_w_in": nrm(ks[9], (N_EVEN, D_MODEL, AB_IN), D_MODEL),
        "pool_w": nrm(ks[10], (N_EVEN, POOL_GROUPS, POOL_CG, POOL_CG), POOL_CG),
        "pool_scale": gain_vec(ks[11], (N_EVEN, POOL_WIDTH)),
        "ret_log_decay": ret_log_decay,
        "ret_norm_g": gain_vec(ks[13], (N_EVEN, RET_V_W)),
        "ab_w_out": nrm(ks[14], (N_EVEN, AB_OUT, D_MODEL), AB_OUT),
        "mla_w_in": nrm(ks[15], (N_ODD, D_MODEL, C_IN), D_MODEL),
        "mla_qa_g": gain_vec(ks[16], (N_ODD, MLA_Q_RANK)),
        "mla_kva_g": gain_vec(ks[17], (N_ODD, MLA_KV_RANK)),
        "mla_w_qb": nrm(ks[18], (N_ODD, MLA_Q_RANK, MLA_HEADS * MLA_QK), MLA_Q_RANK),
        "mla_w_kvb": nrm(ks[19], (N_ODD, MLA_KV_RANK, MLA_HEADS * (MLA_NOPE + MLA_V)), MLA_KV_RANK),
        "mla_qn_g": gain_vec(ks[20], (N_ODD, MLA_QK)),
        "mla_kn_g": gain_vec(ks[21], (N_ODD, MLA_QK)),
        "mla_w_out": nrm(ks[22], (N_ODD, MLA_HEADS * MLA_V, D_MODEL), MLA_HEADS * MLA_V),
    }


def reference(x, c, ctx, c_ctx, ada_w, ada_b, norm_g, ffn_w_in, ffn_w_out,
              ab_w_in, pool_w, pool_scale, ret_log_decay, ret_norm_g, ab_w_out,
              mla_w_in, mla_qa_g, mla_kva_g, mla_w_qb, mla_w_kvb, mla_qn_g, mla_kn_g, mla_w_out):
    L = x.shape[1]
    ROWS = L // GRID_W
    row = jnp.repeat(jnp.arange(ROWS), GRID_W)
    col = jnp.tile(jnp.arange(GRID_W), ROWS)
    rope_ret = axial_rope_table(row, col, RET_DK)
    rope_mla = axial_rope_table(row, col, MLA_ROPE)
    xc = ctx
    for i in range(DEPTH):
        last = i == DEPTH - 1
        j = i // 2
        mod = ada_modulation(c, ada_w[i], ada_b[i])
        mod_c = ada_modulation(c_ctx, ada_w[i], ada_b[i])
        x = macaron_half(x, mod, 0, norm_g[i, 0], ffn_w_in[i, 0], ffn_w_out[i, 0])
        xc = macaron_half(xc, mod_c, 0, norm_g[i, 0], ffn_w_in[i, 0], ffn_w_out[i, 0])
        h = modulate(x, norm_g[i, 1], mod_slice(mod, 3), mod_slice(mod, 4))
        hc = modulate(xc, norm_g[i, 1], mod_slice(mod_c, 3), mod_slice(mod_c, 4))
        if i % 2 == 0:
            y, yc = pool_retention_mixer(h, hc, rope_ret, ab_w_in[j], pool_w[j], pool_scale[j],
                                         ret_log_decay[j], ret_norm_g[j], ab_w_out[j], not last)
        else:
            y, yc = mla_mixer(h, hc, rope_mla, mla_w_in[j], mla_qa_g[j], mla_kva_g[j], mla_w_qb[j],
                              mla_w_kvb[j], mla_qn_g[j], mla_kn_g[j], mla_w_out[j], not last)
        x = x + mod_slice(mod, 5) * y
        x = macaron_half(x, mod, 2, norm_g[i, 2], ffn_w_in[i, 1], ffn_w_out[i, 1])
        if not last:
            xc = xc + mod_slice(mod_c, 5) * yc
            xc = macaron_half(xc, mod_c, 2, norm_g[i, 2], ffn_w_in[i, 1], ffn_w_out[i, 1])
    return x
```

```python
import numpy as np
from contextlib import ExitStack
import concourse.bass as bass
import concourse.mybir as mybir
from concourse.bass_utils import run_bass_kernel_spmd

F32 = mybir.dt.float32
BF16 = mybir.dt.bfloat16
AF = mybir.ActivationFunctionType
ALU = mybir.AluOpType
AX = mybir.AxisListType

D = 1024
NCH = 8
DFF = 2816
NJ = 22
CTX = 256
NT = 256
EPS = 1e-6
DEPTH = 4


class Buf:
    __slots__ = ("w", "r", "name")

    def __init__(self, name=""):
        self.w = None
        self.r = {}
        self.name = name


ENGS = ("pe", "act", "dve", "pool", "sp")


class Prog:
    def __init__(self, nc, n_sp_lanes=10, n_pool_lanes=6, n_act_lanes=0, same_engine_sync=True):
        self.nc = nc
        self.q = {e: [] for e in ENGS}
        self.cnt = {e: 0 for e in ENGS}
        self.lanes = {"sp": [f"sp{i}" for i in range(n_sp_lanes)],
                      "pool": [f"pl{i}" for i in range(n_pool_lanes)],
                      "act": [f"al{i}" for i in range(n_act_lanes)]}
        self.lane_rr = {"sp": 0, "pool": 0, "act": 0}
        self.lane_val = {}
        for q in self.lanes.values():
            for l in q:
                self.lane_val[l] = 0
        self.same = same_engine_sync
        self.n_ops = 0

    def _deps(self, engine, reads, writes):
        waits = {}

        def need(k, v):
            if v > waits.get(k, 0):
                waits[k] = v

        for b in reads:
            if b.w is not None:
                need(*b.w)
        for b in writes:
            if b.w is not None:
                need(*b.w)
            for k, v in b.r.items():
                need(k, v)
        return waits

    def op(self, engine, fn, reads=(), writes=(), inc=True):
        waits = self._deps(engine, reads, writes)
        if engine == "pe" or not self.same:
            waits.pop(engine, None)
        if inc:
            self.cnt[engine] += 1
            c = self.cnt[engine]
        else:
            c = self.cnt[engine] + 1
        self.q[engine].append((waits, fn, engine if inc else None, 1))
        for b in writes:
            b.w = (engine, c)
            b.r = {}
        for b in reads:
            if b.r.get(engine, 0) < c:
                b.r[engine] = c
        self.n_ops += 1

    def dma(self, queue, out, in_, reads=(), writes=()):
        lanes = self.lanes[queue]
        lane = lanes[self.lane_rr[queue] % len(lanes)]
        self.lane_rr[queue] += 1
        waits = self._deps(queue, reads, writes)
        if not self.same:
            waits.pop(queue, None)
        prev = self.lane_val[lane]
        if prev > 0:
            waits[lane] = max(waits.get(lane, 0), prev)
        v = prev + 16
        self.lane_val[lane] = v
        self.q[queue].append((waits, lambda e, o=out, i=in_: e.dma_start(out=o, in_=i), lane, 16))
        for b in writes:
            b.w = (lane, v)
            b.r = {}
        for b in reads:
            if b.r.get(lane, 0) < v:
                b.r[lane] = v
        self.n_ops += 1

    def barrier(self):
        snap = {e: self.cnt[e] for e in ENGS if self.cnt[e] > 0}
        snap.update({l: v for l, v in self.lane_val.items() if v > 0})
        for e in ENGS:
            w = dict(snap)
            w.pop(e, None)
            self.q[e].append((w, None, None, 0))

    def run(self):
        nc = self.nc
        with ExitStack() as es:
            sems = {}
            for k in list(ENGS) + list(self.lane_val.keys()):
                sems[k] = es.enter_context(nc.semaphore("s_" + k))
            block = es.enter_context(nc.Block())

            def replay(ename):
                def body(eng):
                    known = {}
                    for waits, fn, inckey, incval in self.q[ename]:
                        for k, v in waits.items():
                            if known.get(k, 0) < v:
                                eng.wait_ge(sems[k], v)
                                known[k] = v
                        if fn is None:
                            continue
                        ins = fn(eng)
                        if inckey is not None:
                            ins.then_inc(sems[inckey], incval)
                    for l in self.lanes.get(ename, []):
                        if self.lane_val[l] > 0 and known.get(l, 0) < self.lane_val[l]:
                            eng.wait_ge(sems[l], self.lane_val[l])
                return body

            block.tensor(replay("pe"))
            block.scalar(replay("act"))
            block.vector(replay("dve"))
            block.gpsimd(replay("pool"))
            block.sync(replay("sp"))


class Builder:
    def __init__(self, T, plan, debug=False):
        self.debug = debug
        self.T = T
        self.TT = T + CTX
        self.plan = plan
        nc = bass.Bass("TRN2", target_bir_lowering=False)
        self.nc = nc
        self.P = Prog(nc)
        di = lambda name, shape, dt=F32: nc.dram_tensor(name, list(shape), dt, kind="ExternalInput").ap()
        self.x_in = di("x", [T, D])
        self.ctx_in = di("ctx", [CTX, D])
        self.cT_in = di("cT", [128, NCH, 2])
        self.ident_in = di("ident", [128, 128])
        self.ada_w = di("ada_w", [DEPTH, D, 9 * D])
        self.ada_b = di("ada_b", [DEPTH, 128, 72])
        self.norm_g = di("norm_g", [128, DEPTH * 3 * NCH])
        self.ffn_w_in = di("ffn_w_in", [DEPTH, 2, D, 2 * DFF])
        self.ffn_w_out = di("ffn_w_out", [DEPTH, 2, DFF, D])
        self.out = nc.dram_tensor("out", [T, D], F32, kind="ExternalOutput").ap()
        self.xTs = [nc.dram_tensor(f"xT_scr{q}", [D, self.TT], F32, kind="Internal").ap() for q in range(2)]
        self.xcur = 0
        self.tiles = [(t * NT, NT, 0) for t in range(T // NT)] + [(T, CTX, 1)]
        self.xT_bufs = [[Buf(f"xT{q}_{t}") for t in range(len(self.tiles))] for q in range(2)]
        if debug:
            self.out_ctx = nc.dram_tensor("out_ctx", [CTX, D], F32, kind="ExternalOutput").ap()
        self.declare_mixer_inputs(di)
        self.declare_mla_inputs(di)

    def build(self):
        nc, P = self.nc, self.P
        with ExitStack() as es:
            sb = lambda name, shape, dt=F32: es.enter_context(nc.sbuf_tensor("g_" + name, list(shape), dt))
            self.ones_bf = sb("ones_bf", [128, 128], BF16)
            self.ident = sb("ident", [128, 128], F32)
            self.cT = sb("cT", [128, NCH, 2], F32)
            self.silu_c = sb("silu_c", [128, NCH, 32], BF16)
            self.g_sb = sb("g_sb", [128, DEPTH * 3 * NCH], F32)
            self.adab_sb = sb("adab_sb", [128, DEPTH, 72], F32)
            self.modT = sb("modT", [128, DEPTH, 72, 2], F32)
            self.modA = sb("modA", [128, DEPTH, 3, NCH, 2], F32)
            self.modG = sb("modG", [128, DEPTH, 3, NCH, 2], F32)
            self.eps_col = sb("eps_col", [128, 1], F32)
            self.ident_bf = sb("ident_bf", [128, 128], BF16)
            self.b_const = Buf("const")
            self.b_mod = Buf("mod")
            P.op("dve", lambda e: e.memset(self.eps_col[:], EPS), writes=[self.b_const])
            P.op("dve", lambda e: e.memset(self.ones_bf[:], 1.0), writes=[self.b_const])
            P.dma("sp", self.ident[:], self.ident_in[:, :], writes=[self.b_const])
            P.op("dve", lambda e: e.tensor_copy(out=self.ident_bf[:], in_=self.ident[:]), reads=[self.b_const], writes=[self.b_const])
            P.dma("sp", self.cT[:], self.cT_in[:, :, :], writes=[self.b_const])
            P.dma("sp", self.g_sb[:], self.norm_g[:, :], writes=[self.b_const])
            P.dma("sp", self.adab_sb[:], self.ada_b.rearrange("l p n -> p l n"), writes=[self.b_const])
            for pi, ph in enumerate(self.plan):
                P.barrier()
                with ExitStack() as pes:
                    self.sb = lambda name, shape, dt=F32, pi=pi, pes=pes: pes.enter_context(nc.sbuf_tensor(f"s{pi}_{name}", list(shape), dt))
                    self.ps = lambda name, shape, dt=F32, pi=pi, pes=pes: pes.enter_context(nc.psum_tensor(f"p{pi}_{name}", list(shape), dt))
                    kind = ph[0]
                    if kind == "mods":
                        self.phase_mods(ph[1])
                    elif kind == "ffn":
                        self.phase_ffn(*ph[1:])
                    elif kind == "even":
                        self.phase_even(*ph[1:])
                    elif kind == "mla":
                        self.phase_mla(*ph[1:])
                    else:
                        raise ValueError(kind)
            P.barrier()
            P.run()
        return nc

    def phase_mods(self, layers):
        nc, P = self.nc, self.P
        wada = [self.sb(f"wada{b}", [128, NCH, D], BF16) for b in range(2)]
        b_wada = [Buf(f"wada{b}") for b in range(2)]
        psmod = self.ps("psmod", [128, 2, NCH, 16], F32)
        b_ps = [Buf("psmod0"), Buf("psmod1")]
        P.op("dve", lambda e: e.memset(self.silu_c[:], 0.0), writes=[self.b_mod])
        P.op("act", lambda e: e.activation(out=self.silu_c[:, :, 0:2], in_=self.cT[:], func=AF.Silu),
             reads=[self.b_const], writes=[self.b_mod])
        it = 0
        for i in layers:
            for s in range(9):
                bsel = it % 2
                it += 1
                src = self.ada_w[i, :, s * D:(s + 1) * D].rearrange("(kc p) n -> p kc n", p=128)
                P.dma("pool", wada[bsel][:], src, writes=[b_wada[bsel]])
                for m in range(NCH):
                    for kc in range(NCH):
                        last = kc == NCH - 1
                        P.op("pe", lambda e, w=wada[bsel], m=m, kc=kc, bsel=bsel: e.matmul(
                            psmod[:, bsel, m, 0:2], w[:, kc, m * 128:(m + 1) * 128], self.silu_c[:, kc, 0:2],
                            start=(kc == 0), stop=(kc == NCH - 1)),
                            reads=[b_wada[bsel], self.b_mod], writes=[b_ps[bsel]] if (kc == 0 or last) else [],
                            inc=last)
                P.op("dve", lambda e, i=i, s=s, bsel=bsel: e.tensor_tensor(
                    out=self.modT[:, i, s * 8:(s + 1) * 8, :], in0=psmod[:, bsel, :, 0:2],
                    in1=self.adab_sb[:, i, s * 8:(s + 1) * 8].unsqueeze(2).broadcast_to([128, NCH, 2]), op=ALU.add),
                    reads=[b_ps[bsel], self.b_const], writes=[self.b_mod])
            for k in range(3):
                gk = self.g_sb[:, (i * 3 + k) * NCH:(i * 3 + k + 1) * NCH].unsqueeze(2).broadcast_to([128, NCH, 2])
                P.op("dve", lambda e, i=i, k=k, gk=gk: e.scalar_tensor_tensor(
                    out=self.modA[:, i, k, :, :], in0=self.modT[:, i, (3 * k + 1) * 8:(3 * k + 2) * 8, :],
                    scalar=1.0, in1=gk, op0=ALU.add, op1=ALU.mult),
                    reads=[self.b_mod, self.b_const], writes=[self.b_mod])
                P.op("dve", lambda e, i=i, k=k: e.tensor_scalar(
                    self.modG[:, i, k, :, :], self.modT[:, i, (3 * k + 2) * 8:(3 * k + 3) * 8, :],
                    0.5 if k != 1 else 1.0, None, ALU.mult),
                    reads=[self.b_mod], writes=[self.b_mod])

    def load_x_tile(self, ti, xt, b_xt, src, xin=None, b_xin=None, pst=None, b_pst=None, off=0, lo=0, hi=0, part="both"):
        P = self.P
        col0, n, r = self.tiles[ti]
        if src == "scratch":
            if part == "b":
                return
            xT = self.xTs[self.xcur]
            bufs = self.xT_bufs[self.xcur]
            P.dma("sp", xt[:, :, off - lo:off + n + hi], xT[:, col0 - lo:col0 + n + hi].rearrange("(c p) t -> p c t", p=128),
                  reads=[bufs[ti]] + ([bufs[ti - 1]] if lo else []) + ([bufs[ti + 1]] if hi else []), writes=[b_xt])
            return
        rows = self.x_in[col0:col0 + n, :] if r == 0 else self.ctx_in[:, :]
        nb = n // 128
        if part in ("both", "a"):
            P.dma("sp", xin[:, :nb, :], rows.rearrange("(b p) f -> p b f", p=128), writes=[b_xin])
        if part == "a":
            return
        for c in range(NCH):
            for b in range(nb):
                P.op("pe", lambda e, b=b, c=c: e.transpose(
                    pst[:, b * 128:(b + 1) * 128], xin[:, b, c * 128:(c + 1) * 128], self.ident[:]),
                    reads=[b_xin, self.b_const], writes=[b_pst], inc=(b == nb - 1))
            P.op("act", lambda e, c=c: e.copy(out=xt[:, c, off:off + n], in_=pst[:, :n]), reads=[b_pst], writes=[b_xt])

    def store_x_tile(self, ti, xt, b_xt, dst, ot=None, b_ot=None, pso=None, b_pso=None, off=0, flip=False):
        P = self.P
        col0, n, r = self.tiles[ti]
        if dst == "scratch":
            idx = (1 - self.xcur) if flip else self.xcur
            P.dma("sp", self.xTs[idx][:, col0:col0 + n].rearrange("(c p) t -> p c t", p=128), xt[:, :, off:off + n],
                  reads=[b_xt], writes=[self.xT_bufs[idx][ti]])
            return
        if r == 1 and not self.debug:
            return
        target = self.out if r == 0 else self.out_ctx
        nb = n // 128
        for b in range(nb):
            for half in range(2):
                for cc in range(4):
                    c = half * 4 + cc
                    P.op("pe", lambda e, b=b, c=c, cc=cc, half=half: e.transpose(
                        pso[half][:, cc * 128:(cc + 1) * 128], xt[:, c, off + b * 128:off + (b + 1) * 128], self.ident[:]),
                        reads=[b_xt, self.b_const], writes=[b_pso[half]], inc=(cc == 3))
                P.op("act", lambda e, b=b, half=half: e.copy(out=ot[:, b, half * 512:(half + 1) * 512], in_=pso[half][:, :]),
                     reads=[b_pso[half]], writes=[b_ot])
        c0 = col0 if r == 0 else 0
        P.dma("sp", target[c0:c0 + n, :].rearrange("(b p) f -> p b f", p=128), ot[:, :nb, :], reads=[b_ot])

    def norm_steps(self, xt, b_xt, n, i, k, r, h, b_h, W, off=0):
        P = self.P
        sq, b_sq, pss, b_pss, rt, b_rt, rstd, b_rstd, tmp, b_tmp = W
        cs = slice(off, off + n)
        steps = []
        steps.append(lambda: P.op("act", lambda e: e.activation(out=sq[:, :, cs], in_=xt[:, :, cs], func=AF.Square), reads=[b_xt], writes=[b_sq]))

        def ss():
            for c in range(NCH):
                P.op("pe", lambda e, c=c: e.matmul(pss[:, :n], self.ones_bf[:], sq[:, c, cs], start=(c == 0), stop=(c == NCH - 1)),
                     reads=[b_sq, self.b_const], writes=[b_pss] if c in (0, NCH - 1) else [], inc=(c == NCH - 1))
        steps.append(ss)

        def rs():
            P.op("act", lambda e: e.activation(out=rt[:, :n], in_=pss[:, :n], func=AF.Sqrt, bias=self.eps_col[:, 0:1], scale=1.0 / D),
                 reads=[b_pss, self.b_const], writes=[b_rt])
            P.op("dve", lambda e: e.reciprocal(out=rstd[:, :n], in_=rt[:, :n]), reads=[b_rt], writes=[b_rstd])
        steps.append(rs)
        for half in range(2):
            hs_ = slice(half * 4, half * 4 + 4)
            steps.append(lambda hs_=hs_: P.op("dve", lambda e: e.tensor_tensor(
                out=tmp[:, hs_, cs], in0=xt[:, hs_, cs], in1=rstd[:, :n].unsqueeze(1).broadcast_to([128, 4, n]), op=ALU.mult),
                reads=[b_xt, b_rstd], writes=[b_tmp]))
        for c in range(NCH):
            steps.append(lambda c=c: P.op("act", lambda e: e.activation(
                out=h[:, c, cs], in_=tmp[:, c, cs], func=AF.Identity, bias=self.modT[:, i, 3 * k * 8 + c, r:r + 1],
                scale=self.modA[:, i, k, c, r:r + 1]), reads=[b_tmp, self.b_mod], writes=[b_h]))
        return steps

    def norm_tile(self, xt, b_xt, n, i, k, r, h, b_h, W, off=0):
        for st in self.norm_steps(xt, b_xt, n, i, k, r, h, b_h, W, off):
            st()

    def norm_work(self, tag=""):
        sb, ps = self.sb, self.ps
        return (sb("sq" + tag, [128, NCH, NT], BF16), Buf("sq"), ps("pss" + tag, [128, 512], F32), Buf("pss"),
                sb("rt" + tag, [128, NT], F32), Buf("rt"), sb("rstd" + tag, [128, NT], F32), Buf("rstd"),
                sb("tmp" + tag, [128, NCH, NT], F32), Buf("tmp"))

    def phase_ffn(self, i, kf, src, dst, skip_ctx=False):
        nc, P, sb, ps = self.nc, self.P, self.sb, self.ps
        k = 0 if kf == 0 else 2
        w_in = sb("w_in", [128, NCH, 2 * DFF], BF16)
        w_out = sb("w_out", [128, NJ, D], BF16)
        jr = [(0, 6), (6, 12), (12, 17), (17, 22)]
        piece_of = {}
        for q, (j0, j1) in enumerate(jr):
            for j in range(j0, j1):
                piece_of[j] = q
        b_wa = [Buf("wa%d" % q) for q in range(4)]
        b_wb = [Buf("wb%d" % q) for q in range(4)]
        b_wo = [Buf("wo%d" % q) for q in range(4)]
        for q, (j0, j1) in enumerate(jr):
            for half, bb in ((0, b_wa), (1, b_wb)):
                cs = slice(half * DFF + j0 * 128, half * DFF + j1 * 128)
                P.dma("pool", w_in[:, :, cs], self.ffn_w_in[i, kf, :, cs].rearrange("(kc p) n -> p kc n", p=128), writes=[bb[q]])
            P.dma("pool", w_out[:, j0:j1, :], self.ffn_w_out[i, kf, j0 * 128:j1 * 128, :].rearrange("(j p) n -> p j n", p=128), writes=[b_wo[q]])
        xts = [sb(f"xt{b}", [128, NCH, NT], F32) for b in range(2)]
        b_xts = [Buf(f"xt{b}") for b in range(2)]
        W = self.norm_work()
        h = sb("h", [128, NCH, NT], BF16)
        b_h = Buf("h")
        sa = [sb(f"sa{b}", [128, NT], F32) for b in range(2)]
        b_sa = [Buf("sa") for _ in range(2)]
        u = [sb(f"u{b}", [128, NT], BF16) for b in range(3)]
        b_u = [Buf("u") for _ in range(3)]
        psab = ps("psab", [128, 2, 2, NT], F32)
        b_psa = [Buf("psa") for _ in range(2)]
        b_psb = [Buf("psb") for _ in range(2)]
        psy = ps("psy", [128, NCH, NT], F32)
        b_psy = Buf("psy")
        xin = b_xin = pst = b_pst = ot = b_ot = pso = b_pso = None
        if src == "input" or dst == "output":
            pst, b_pst = ps("pst", [128, 512], F32), Buf("pst")
        if src == "input":
            xin, b_xin = sb("xin", [128, 2, D], F32), Buf("xin")
        if dst == "output":
            ot, b_ot = sb("ot", [128, 2, D], F32), Buf("ot")
            pso, b_pso = [pst, pst], [b_pst, b_pst]

        hs = [h, sb("h1", [128, NCH, NT], BF16)]
        b_hs = [b_h, Buf("h1")]
        tl = [ti for ti in range(len(self.tiles)) if not (self.tiles[ti][2] == 1 and skip_ctx)]

        def prep_a(idx):
            ti = tl[idx]
            self.load_x_tile(ti, xts[idx % 2], b_xts[idx % 2], src, xin, b_xin, pst, b_pst, part="a")

        def prep_b(idx):
            ti = tl[idx]
            self.load_x_tile(ti, xts[idx % 2], b_xts[idx % 2], src, xin, b_xin, pst, b_pst, part="b")

        def prep_norm(idx):
            ti = tl[idx]
            col0, n, r = self.tiles[ti]
            self.norm_tile(xts[idx % 2], b_xts[idx % 2], n, i, k, r, hs[idx % 2], b_hs[idx % 2], W)

        def prep_norm_steps(idx):
            ti = tl[idx]
            col0, n, r = self.tiles[ti]
            return self.norm_steps(xts[idx % 2], b_xts[idx % 2], n, i, k, r, hs[idx % 2], b_hs[idx % 2], W)

        def do_tile(idx):
            ti = tl[idx]
            col0, n, r = self.tiles[ti]
            xt, b_xt = xts[idx % 2], b_xts[idx % 2]
            hh, b_hh = hs[idx % 2], b_hs[idx % 2]
            has_next = idx + 1 < len(tl)
            if has_next:
                prep_a(idx + 1)

            def ab(j):
                bsel = j % 2
                for half in (0, 1):
                    for kc in range(NCH):
                        cs = slice(half * DFF + j * 128, half * DFF + (j + 1) * 128)
                        P.op("pe", lambda e, kc=kc, cs=cs, half=half, bsel=bsel: e.matmul(
                            psab[:, bsel, half, :n], w_in[:, kc, cs], hh[:, kc, :n], start=(kc == 0), stop=(kc == NCH - 1)),
                            reads=[b_hh, b_wa[piece_of[j]], b_wb[piece_of[j]]], writes=[b_psa[bsel]] if kc in (0, NCH - 1) else [], inc=(kc == NCH - 1))
                P.op("act", lambda e, bsel=bsel: e.activation(out=sa[bsel][:, :n], in_=psab[:, bsel, 0, :n], func=AF.Silu),
                     reads=[b_psa[bsel]], writes=[b_sa[bsel]])
                P.op("dve", lambda e, bsel=bsel, j=j: e.tensor_tensor(out=u[j % 3][:, :n], in0=sa[bsel][:, :n],
                                                                      in1=psab[:, bsel, 1, :n], op=ALU.mult),
                     reads=[b_sa[bsel], b_psa[bsel]], writes=[b_u[j % 3]])

            def yacc(j):
                for m in range(NCH):
                    P.op("pe", lambda e, m=m, j=j: e.matmul(
                        psy[:, m, :n], w_out[:, j, m * 128:(m + 1) * 128], u[j % 3][:, :n], start=(j == 0 and m % 2 == 0),
                        stop=(j == NJ - 1), skip_group_check=True),
                        reads=[b_u[j % 3], b_wo[piece_of[j]]], writes=[b_psy] if (j in (0, NJ - 1)) else [],
                        inc=(m == NCH - 1))

            ab(0)
            nsteps = []
            for j in range(1, NJ):
                ab(j)
                yacc(j - 1)
                if has_next and j == 4:
                    prep_b(idx + 1)
                if has_next and j == 6:
                    nsteps = prep_norm_steps(idx + 1)
                if j >= 7 and nsteps:
                    nsteps.pop(0)()
            while nsteps:
                nsteps.pop(0)()
            yacc(NJ - 1)
            for m in range(NCH):
                P.op("dve", lambda e, m=m: e.scalar_tensor_tensor(
                    out=xt[:, m, :n], in0=psy[:, m, :n], scalar=self.modG[:, i, k, m, r:r + 1], in1=xt[:, m, :n],
                    op0=ALU.mult, op1=ALU.add),
                    reads=[b_psy, b_xt, self.b_mod], writes=[b_xt])
            self.store_x_tile(ti, xt, b_xt, dst, ot, b_ot, pso, b_pso)

        prep_a(0)
        prep_b(0)
        prep_norm(0)
        for idx in range(len(tl)):
            do_tile(idx)
        if self.debug:
            d_h = nc.dram_tensor(f"dbg_h{i}{kf}", [128, NCH * NT], F32, kind="ExternalOutput").ap()
            P.dma("pool", d_h[:, :], h[:].rearrange("p c t -> p (c t)"), reads=[b_h, b_hs[1]])
            d_r = nc.dram_tensor(f"dbg_r{i}{kf}", [128, NT], F32, kind="ExternalOutput").ap()
            P.dma("sp", d_r[:, :], W[6][:, :], reads=[W[7]])
            d_u = nc.dram_tensor(f"dbg_u{i}{kf}", [128, NT], F32, kind="ExternalOutput").ap()
            P.dma("pool", d_u[:, :], u[(NJ - 1) % 3][:, :], reads=[b_u[(NJ - 1) % 3]])


    def declare_mixer_inputs(self, di):
        T = self.T
        self.ab_w_in = di("ab_w_in", [2, D, 2560])
        self.ab_w_sw = di("ab_w_sw", [2, D, 768])
        self.ab_w_out = di("ab_w_out", [2, D, D])
        self.pool_w = di("pool_w", [2, 4, 64, 64])
        self.pool_scale_in = di("pool_scale", [128, 4])
        self.ret_norm_g_in = di("ret_norm_g", [128, 12])
        self.lgcol_in = di("lgcol", [128, 2, 2, 3])
        self.lgbc_in = di("lgbc", [128, 24])
        self.rtab_in = di("rtab", [8, 128, 128])
        self.rope_r_in = di("rope_r", [2, 128, T])
        self.invc_in = di("invc", [2, 128, T + CTX])
        nchk = T // 128 + 2
        self.sb_scr = self.nc.dram_tensor("sb_scr", [nchk, 128, 384], BF16, kind="Internal").ap()
        self.sb_bufs = [Buf(f"sbscr{q}") for q in range(nchk)]

    def phase_even(self, i, src="scratch", dst="scratch"):
        nc, P, sb, ps = self.nc, self.P, self.sb, self.ps
        jl = i // 2
        T = self.T
        E = 8
        NW = NT + 2 * E
        NQK = 3
        w_ab = sb("w_ab", [128, NCH, 3328], BF16)
        b_wab = Buf("w_ab")
        for q in range(2):
            cs = slice(q * 1280, (q + 1) * 1280)
            P.dma("pool", w_ab[:, :, cs], self.ab_w_in[jl, :, cs].rearrange("(kc p) n -> p kc n", p=128), writes=[b_wab])
        P.dma("pool", w_ab[:, :, 2560:3328], self.ab_w_sw[jl, :, :].rearrange("(kc p) n -> p kc n", p=128), writes=[b_wab])
        w_abo = sb("w_abo", [128, NCH, D], BF16)
        b_wabo = Buf("w_abo")
        P.dma("pool", w_abo[:, :, :], self.ab_w_out[jl, :, :].rearrange("(kc p) n -> p kc n", p=128), writes=[b_wabo])
        wblk = sb("wblk", [128, 2, 128], BF16)
        b_tab = Buf("tables")
        P.op("dve", lambda e: e.memset(wblk[:], 0.0), writes=[b_tab])
        for g in range(4):
            P.dma("pool", wblk[(g % 2) * 64:(g % 2) * 64 + 64, g // 2, (g % 2) * 64:(g % 2) * 64 + 64], self.pool_w[jl, g, :, :], writes=[b_tab])
        rtab = sb("rtab", [128, 8, 128], F32)
        P.dma("sp", rtab[:], self.rtab_in.rearrange("a p n -> p a n"), writes=[b_tab])
        lgcol = sb("lgcol", [128, 2, 3], F32)
        P.dma("sp", lgcol[:], self.lgcol_in[:, jl, :, :], writes=[b_tab])
        lgbc = sb("lgbc", [128, 12], F32)
        P.dma("sp", lgbc[:], self.lgbc_in[:, jl * 12:(jl + 1) * 12], writes=[b_tab])
        pscale = sb("pscale", [128, 2], F32)
        P.dma("sp", pscale[:], self.pool_scale_in[:, jl * 2:(jl + 1) * 2], writes=[b_tab])
        rng = sb("rng", [128, 6], F32)
        P.dma("sp", rng[:], self.ret_norm_g_in[:, jl * 6:(jl + 1) * 6], writes=[b_tab])
        gC = sb("gC", [128, 2, 3], F32)
        P.op("act", lambda e: e.activation(out=gC[:], in_=lgcol[:], func=AF.Exp, scale=128.0), reads=[b_tab], writes=[b_tab])
        DT = sb("DT", [128, 6, 128], BF16)
        dtmp = [sb(f"dtmp{q}", [128, 128], F32) for q in range(2)]
        for hh in range(6):
            P.op("act", lambda e, hh=hh: e.activation(out=dtmp[0][:], in_=rtab[:, 0, :], func=AF.Exp, scale=lgbc[:, hh:hh + 1]), reads=[b_tab], writes=[b_tab])
            P.op("act", lambda e, hh=hh: e.activation(out=dtmp[1][:], in_=rtab[:, 1, :], func=AF.Exp, scale=lgbc[:, 6 + hh:7 + hh]), reads=[b_tab], writes=[b_tab])
            P.op("dve", lambda e: e.tensor_tensor(out=dtmp[0][:], in0=dtmp[0][:], in1=rtab[:, 2, :], op=ALU.mult), reads=[b_tab], writes=[b_tab])
            P.op("dve", lambda e: e.tensor_tensor(out=dtmp[1][:], in0=dtmp[1][:], in1=rtab[:, 3, :], op=ALU.mult), reads=[b_tab], writes=[b_tab])
            P.op("dve", lambda e: e.tensor_tensor(out=dtmp[0][:], in0=dtmp[0][:], in1=dtmp[1][:], op=ALU.add), reads=[b_tab], writes=[b_tab])
            P.op("dve", lambda e, hh=hh: e.tensor_scalar(DT[:, hh, :], dtmp[0][:], 0.125, None, ALU.mult), reads=[b_tab], writes=[b_tab])
        XI = sb("XI", [128, 4, 3, 128], F32)
        for a in range(4):
            d = a % 2
            for pr in range(3):
                P.op("act", lambda e, a=a, d=d, pr=pr: e.activation(out=XI[:, a, pr, :], in_=rtab[:, 4 + a, :], func=AF.Exp, scale=lgcol[:, d, pr:pr + 1]),
                     reads=[b_tab], writes=[b_tab])
        P.op("dve", lambda e: e.tensor_scalar(XI[:, 2:4, :, :], XI[:, 2:4, :, :], 0.125, None, ALU.mult), reads=[b_tab], writes=[b_tab])
        xts = [sb(f"xt{b}", [128, NCH, NW], F32) for b in range(2)]
        b_xts = [Buf(f"xt{b}") for b in range(2)]
        Wn = (sb("sq", [128, NCH, NW], BF16), Buf("sq"), ps("pss", [128, 512], F32), Buf("pss"),
              sb("rt", [128, NW], F32), Buf("rt"), sb("rstd", [128, NW], F32), Buf("rstd"),
              sb("tmp", [128, NCH, NW], F32), Buf("tmp"))
        h = sb("h", [128, NCH, NW], BF16)
        b_h = Buf("h")
        pp = [ps(f"pp{q}", [128, 512], F32) for q in range(2)]
        b_pp = [Buf(f"pp{q}") for q in range(2)]
        ppi = [0]
        pS = [ps(f"pS{q}", [128, 512], F32) for q in range(2)]
        b_pS = [Buf(f"pS{q}") for q in range(2)]
        pR = ps("pR", [128, 512], F32)
        b_pR = Buf("pR")
        pK = ps("pK", [128, 2, 256], F32)
        b_pK = [Buf("pK")] * 2
        pT = ps("pT", [128, 3, 128], BF16)
        b_pT = Buf("pT")
        xin, b_xin = (sb("xin", [128, 2, D], F32), Buf("xin")) if src == "input" else (None, None)
        ot, b_ot = (sb("ot", [128, 2, D], F32), Buf("ot")) if dst == "output" else (None, None)
        pst, b_pst = pR, b_pR
        rope = sb("rope", [128, 2, NT], F32)
        b_rope = Buf("rope")
        kr = sb("kr", [128, NQK, NT], F32)
        b_kr = Buf("kr")
        qr = sb("qr", [128, NQK, NT], F32)
        b_qr = Buf("qr")
        t1 = sb("t1", [128, NT], F32)
        b_t1 = Buf("t1")
        k_bf = sb("k_bf", [128, NQK, NT], BF16)
        q_bf = sb("q_bf", [128, NQK, NT], BF16)
        qf_bf = sb("qf_bf", [128, NQK, NT], BF16)
        qb_bf = sb("qb_bf", [128, NQK, NT], BF16)
        b_qk = Buf("qkbf")
        khat = sb("khat", [128, NQK, 128], BF16)
        b_khat = Buf("khat")
        kt_sb = sb("kt_sb", [128, NQK, 128], BF16)
        b_kt = Buf("kt")
        v_sb = sb("v_sb", [128, 2, 768], BF16)
        b_v = Buf("v")
        S32 = [sb(f"S32_{d}", [128, 3, 128], F32) for d in range(2)]
        Sbf = sb("Sbf", [128, 3, 128], BF16)
        b_S = [Buf("Sf"), Buf("Sb")]
        b_Sbf = Buf("Sbf")
        sbt = [sb(f"sbt{q}", [128, 3, 128], BF16) for q in range(2)]
        b_sbt = [Buf("sbt0"), Buf("sbt1")]
        sbo = [sb(f"sbo{q}", [128, 3, 128], BF16) for q in range(2)]
        b_sbo = [Buf("sbo0"), Buf("sbo1")]
        mT = [sb(f"mT{q}", [128, 128], BF16) for q in range(2)]
        b_mT = [Buf("mT0"), Buf("mT1")]
        retT = sb("retT", [128, 6, NT], F32)
        b_ret = Buf("retT")
        gs = sb("gs", [128, 6, NT], F32)
        b_gs = Buf("gs")
        yT = sb("yT", [128, NCH, NT], BF16)
        b_yT = Buf("yT")
        pext = sb("pext", [128, 2, NW], F32)
        b_pext = Buf("pext")
        pa = [sb(f"pa{q}", [128, NW], F32) for q in range(3)]
        b_pa = Buf("pa")
        invc = sb("invc", [128, 2, NT], F32)
        b_invc = Buf("invc")
        rsq = sb("rsq", [128, NT], BF16)
        b_rsq = Buf("rsq")
        rr = sb("rr", [128, NT], F32)
        b_rr = Buf("rr")
        nchk_lat = T // 128

        def next_pp():
            q = ppi[0] % 2
            ppi[0] += 1
            return pp[q], b_pp[q]

        def proj_fm(col0w, width, h_lo, n):
            pt, b_pt = next_pp()
            for kc in range(NCH):
                P.op("pe", lambda e, kc=kc, pt=pt: e.matmul(pt[:width, :n], w_ab[:, kc, col0w:col0w + width], h[:, kc, h_lo:h_lo + n],
                                                           start=(kc == 0), stop=(kc == NCH - 1)),
                     reads=[b_h, b_wab], writes=[b_pt] if kc in (0, NCH - 1) else [], inc=(kc == NCH - 1))
            return pt, b_pt

        def load_tile(ti, xt, b_xt, halo):
            col0, n, r = self.tiles[ti]
            lo = E if (halo and src == "scratch" and r == 0 and col0 > 0) else 0
            hi = E if (halo and src == "scratch" and r == 0 and col0 + n < T) else 0
            self.load_x_tile(ti, xt, b_xt, src, xin, b_xin, pst, b_pst, off=E, lo=lo, hi=hi)
            return lo, hi

        def qk_rope(col_w, col_sw, dst32, b_dst, n, r):
            for c in range(NQK):
                pt, b_pt = proj_fm(col_w + c * 128, 128, E, n)
                if r == 1:
                    P.op("act", lambda e, c=c, pt=pt: e.copy(out=dst32[:, c, :n], in_=pt[:, :n]), reads=[b_pt], writes=[b_dst])
                    continue
                pt2, b_pt2 = proj_fm(col_sw + c * 128, 128, E, n)
                P.op("dve", lambda e, c=c, pt=pt: e.tensor_tensor(out=dst32[:, c, :n], in0=pt[:, :n], in1=rope[:, 0, :n], op=ALU.mult),
                     reads=[b_pt, b_rope], writes=[b_dst])
                P.op("dve", lambda e, pt2=pt2: e.tensor_tensor(out=t1[:, :n], in0=pt2[:, :n], in1=rope[:, 1, :n], op=ALU.mult),
                     reads=[b_pt2, b_rope], writes=[b_t1])
                P.op("dve", lambda e, c=c: e.tensor_tensor(out=dst32[:, c, :n], in0=dst32[:, c, :n], in1=t1[:, :n], op=ALU.add),
                     reads=[b_t1, b_dst], writes=[b_dst])

        def v_proj(n):
            for blk in range(n // 128):
                for half in range(2):
                    pt, b_pt = next_pp()
                    for kc in range(NCH):
                        P.op("pe", lambda e, kc=kc, pt=pt, blk=blk, half=half: e.matmul(
                            pt[:, :384], h[:, kc, E + blk * 128:E + (blk + 1) * 128], w_ab[:, kc, 1024 + half * 384:1024 + (half + 1) * 384],
                            start=(kc == 0), stop=(kc == NCH - 1)),
                            reads=[b_h, b_wab], writes=[b_pt] if kc in (0, NCH - 1) else [], inc=(kc == NCH - 1))
                    P.op("act", lambda e, pt=pt, blk=blk, half=half: e.copy(out=v_sb[:, blk, half * 384:(half + 1) * 384], in_=pt[:, :384]),
                         reads=[b_pt], writes=[b_v])

        def state_update(d, blk, zidx):
            P.op("dve", lambda e: e.tensor_tensor(out=khat[:, :, :], in0=kr[:, :, blk * 128:(blk + 1) * 128], in1=XI[:, zidx, :, :], op=ALU.mult),
                 reads=[b_kr, b_tab], writes=[b_khat])
            for c in range(NQK):
                P.op("pe", lambda e, c=c: e.transpose(pT[:, c, :], khat[:, c, :], self.ident_bf[:]),
                     reads=[b_khat, self.b_const], writes=[b_pT], inc=(c == NQK - 1))
            P.op("act", lambda e: e.copy(out=kt_sb[:], in_=pT[:]), reads=[b_pT], writes=[b_kt])
            for c in range(NQK):
                rg = c % 2
                P.op("pe", lambda e, c=c, rg=rg: e.matmul(pK[:, rg, :], kt_sb[:, c, :], v_sb[:, blk, c * 256:(c + 1) * 256], start=True, stop=True),
                     reads=[b_kt, b_v], writes=[b_pK[rg]])
                for hf in range(2):
                    prt = slice(hf * 64, hf * 64 + 64)
                    P.op("dve", lambda e, c=c, rg=rg, hf=hf, prt=prt: e.scalar_tensor_tensor(
                        out=S32[d][prt, c, :], in0=S32[d][prt, c, :], scalar=gC[prt, d, c:c + 1],
                        in1=pK[prt, rg, hf * 128:(hf + 1) * 128], op0=ALU.mult, op1=ALU.add),
                        reads=[b_pK[rg], b_S[d], b_tab], writes=[b_S[d]])

        P.op("dve", lambda e: e.memset(S32[1][:], 0.0), writes=[b_S[1]])
        P.op("dve", lambda e: e.memset(S32[0][:], 0.0), writes=[b_S[0]])
        ntl = len(self.tiles)
        order1 = [ntl - 1] + list(range(ntl - 2, -1, -1))
        sbo_i = [0]

        def sweep1_tile(ti, xt, b_xt):
            col0, n, r = self.tiles[ti]
            self.norm_tile(xt, b_xt, n, i, 1, r, h, b_h, Wn, off=E)
            if r == 0:
                P.dma("sp", rope[:, :, :n], self.rope_r_in[:, :, col0:col0 + n].rearrange("a p t -> p a t"), writes=[b_rope])
            qk_rope(640, 2560 + 384, kr, b_kr, n, r)
            v_proj(n)
            for blk in range(n // 128 - 1, -1, -1):
                gchunk = (col0 // 128 + blk) if r == 0 else (nchk_lat + blk)
                q = sbo_i[0] % 2
                sbo_i[0] += 1
                P.op("act", lambda e, q=q: e.copy(out=sbo[q][:], in_=S32[1][:]), reads=[b_S[1]], writes=[b_sbo[q]])
                P.dma("sp", self.sb_scr[gchunk, :, :], sbo[q][:].rearrange("p a b -> p (a b)"), reads=[b_sbo[q]], writes=[self.sb_bufs[gchunk]])
                state_update(1, blk, 3)

        load_tile(order1[0], xts[0], b_xts[0], False)
        for n_, ti in enumerate(order1):
            if n_ + 1 < len(order1):
                load_tile(order1[n_ + 1], xts[(n_ + 1) % 2], b_xts[(n_ + 1) % 2], False)
            sweep1_tile(ti, xts[n_ % 2], b_xts[n_ % 2])

        order2 = [ntl - 1] + list(range(0, ntl - 1))
        sbt_i = [0]

        def sweep2_tile(ti, xt, b_xt, lo, hi):
            col0, n, r = self.tiles[ti]
            w0 = E - lo
            wn = lo + n + hi
            self.norm_tile(xt, b_xt, wn, i, 1, r, h, b_h, Wn, off=w0)
            if r == 0:
                P.dma("sp", rope[:, :, :n], self.rope_r_in[:, :, col0:col0 + n].rearrange("a p t -> p a t"), writes=[b_rope])
            P.op("pool", lambda e: e.memset(pext[:], 0.0), writes=[b_pext])
            for c in range(2):
                pt, b_pt = proj_fm(c * 128, 128, w0, wn)
                P.op("act", lambda e, c=c, pt=pt: e.copy(out=pext[:, c, w0:w0 + wn], in_=pt[:, :wn]), reads=[b_pt], writes=[b_pext])
            P.dma("sp", invc[:, :, :n], self.invc_in[:, :, col0:col0 + n].rearrange("a p t -> p a t"), writes=[b_invc])
            for g in range(4):
                w = (2, 4, 8, 16)[g]
                c, p0 = g // 2, (g % 2) * 64
                prt = slice(p0, p0 + 64)
                cur = pext[prt, c, :]
                width = NW
                lvl = 1
                bi = 0
                while lvl * 2 < w:
                    width -= lvl
                    dstb = pa[bi % 2]
                    P.op("dve", lambda e, cur=cur, dstb=dstb, width=width, lvl=lvl, prt=prt: e.tensor_tensor(
                        out=dstb[prt, :width], in0=cur[:, 0:width], in1=cur[:, lvl:lvl + width], op=ALU.add),
                        reads=[b_pext, b_pa], writes=[b_pa])
                    cur = dstb[prt, :]
                    lvl *= 2
                    bi += 1
                hw = w // 2
                P.op("dve", lambda e, cur=cur, hw=hw, prt=prt: e.tensor_tensor(
                    out=pa[2][prt, :n], in0=cur[:, E - hw:E - hw + n], in1=cur[:, E:E + n], op=ALU.add),
                    reads=[b_pext, b_pa], writes=[b_pa])
                P.op("dve", lambda e, prt=prt, c=c: e.tensor_tensor(out=pa[2][prt, :n], in0=pa[2][prt, :n], in1=invc[prt, c, :n], op=ALU.mult),
                     reads=[b_pa, b_invc], writes=[b_pa])
                P.op("dve", lambda e, prt=prt, c=c: e.tensor_tensor(out=yT[prt, c, :n], in0=pa[2][prt, :n], in1=pext[prt, c, E:E + n], op=ALU.subtract),
                     reads=[b_pa, b_pext], writes=[b_yT])
            for c in range(2):
                pt, b_pt = next_pp()
                P.op("pe", lambda e, c=c, pt=pt: e.matmul(pt[:, :n], wblk[:, c, :], yT[:, c, :n], start=True, stop=True),
                     reads=[b_yT, b_tab], writes=[b_pt])
                P.op("dve", lambda e, c=c, pt=pt: e.tensor_scalar(yT[:, c, :n], pt[:, :n], pscale[:, c:c + 1], None, ALU.mult),
                     reads=[b_pt, b_tab], writes=[b_yT])
            qk_rope(256, 2560, qr, b_qr, n, r)
            qk_rope(640, 2560 + 384, kr, b_kr, n, r)
            v_proj(n)
            P.op("act", lambda e: e.copy(out=k_bf[:, :, :n], in_=kr[:, :, :n]), reads=[b_kr], writes=[b_qk])
            P.op("act", lambda e: e.copy(out=q_bf[:, :, :n], in_=qr[:, :, :n]), reads=[b_qr], writes=[b_qk])
            for blk in range(n // 128):
                bs = slice(blk * 128, (blk + 1) * 128)
                P.op("dve", lambda e, bs=bs: e.tensor_tensor(out=qf_bf[:, :, bs], in0=qr[:, :, bs], in1=XI[:, 0, :, :], op=ALU.mult),
                     reads=[b_qr, b_tab], writes=[b_qk])
                P.op("dve", lambda e, bs=bs: e.tensor_tensor(out=qb_bf[:, :, bs], in0=qr[:, :, bs], in1=XI[:, 1, :, :], op=ALU.mult),
                     reads=[b_qr, b_tab], writes=[b_qk])
            for c in range(6):
                pt, b_pt = proj_fm(1792 + c * 128, 128, E, n)
                P.op("act", lambda e, c=c, pt=pt: e.activation(out=gs[:, c, :n], in_=pt[:, :n], func=AF.Silu), reads=[b_pt], writes=[b_gs])
            for blk in range(n // 128):
                gchunk = (col0 // 128 + blk) if r == 0 else (nchk_lat + blk)
                P.dma("sp", sbt[blk % 2][:].rearrange("p a b -> p (a b)"), self.sb_scr[gchunk, :, :], reads=[self.sb_bufs[gchunk]], writes=[b_sbt[blk % 2]])
            for blk in range(n // 128):
                self._even_chunk(blk, col0, r, nchk_lat, sbt, b_sbt, sbt_i, Sbf, b_Sbf, S32, b_S, pS, b_pS, k_bf, q_bf, qf_bf, qb_bf, b_qk,
                                 mT, b_mT, DT, b_tab, pR, b_pR, v_sb, b_v, retT, b_ret)
                state_update(0, blk, 2)
            for hh in range(6):
                P.op("act", lambda e, hh=hh: e.activation(out=rsq[:, :n], in_=retT[:, hh, :n], func=AF.Square), reads=[b_ret], writes=[b_rsq])
                pt, b_pt = next_pp()
                P.op("pe", lambda e, pt=pt: e.matmul(pt[:, :n], self.ones_bf[:], rsq[:, :n], start=True, stop=True),
                     reads=[b_rsq, self.b_const], writes=[b_pt])
                P.op("act", lambda e, pt=pt: e.activation(out=rr[:, :n], in_=pt[:, :n], func=AF.Sqrt, bias=self.eps_col[:, 0:1], scale=1.0 / 128),
                     reads=[b_pt, self.b_const], writes=[b_rr])
                P.op("dve", lambda e: e.reciprocal(out=rr[:, :n], in_=rr[:, :n]), reads=[b_rr], writes=[b_rr])
                P.op("dve", lambda e, hh=hh: e.scalar_tensor_tensor(out=retT[:, hh, :n], in0=retT[:, hh, :n], scalar=rng[:, hh:hh + 1], in1=rr[:, :n],
                                                                   op0=ALU.mult, op1=ALU.mult), reads=[b_ret, b_rr, b_tab], writes=[b_ret])
                P.op("dve", lambda e, hh=hh: e.tensor_tensor(out=yT[:, 2 + hh, :n], in0=retT[:, hh, :n], in1=gs[:, hh, :n], op=ALU.mult),
                     reads=[b_ret, b_gs], writes=[b_yT])
            for m in range(NCH):
                pt, b_pt = next_pp()
                for c in range(NCH):
                    P.op("pe", lambda e, m=m, c=c, pt=pt: e.matmul(pt[:, :n], w_abo[:, c, m * 128:(m + 1) * 128], yT[:, c, :n], start=(c == 0), stop=(c == NCH - 1)),
                         reads=[b_yT, b_wabo], writes=[b_pt] if c in (0, NCH - 1) else [], inc=(c == NCH - 1))
                P.op("dve", lambda e, m=m, pt=pt: e.scalar_tensor_tensor(out=xt[:, m, E:E + n], in0=pt[:, :n], scalar=self.modG[:, i, 1, m, r:r + 1],
                                                                        in1=xt[:, m, E:E + n], op0=ALU.mult, op1=ALU.add),
                     reads=[b_pt, b_xt, self.b_mod], writes=[b_xt])
            self.store_x_tile(ti, xt, b_xt, dst, ot, b_ot, [pst, pst], [b_pst, b_pst], off=E, flip=True)

        nbase = len(order1)
        lh = {}
        lh[0] = load_tile(order2[0], xts[nbase % 2], b_xts[nbase % 2], True)
        for n_, ti in enumerate(order2):
            if n_ + 1 < len(order2):
                lh[n_ + 1] = load_tile(order2[n_ + 1], xts[(nbase + n_ + 1) % 2], b_xts[(nbase + n_ + 1) % 2], True)
            sweep2_tile(ti, xts[(nbase + n_) % 2], b_xts[(nbase + n_) % 2], *lh[n_])
        if dst == "scratch":
            self.xcur = 1 - self.xcur

    def _even_chunk(self, blk, col0, r, nchk_lat, sbt, b_sbt, sbt_i, Sbf, b_Sbf, S32, b_S, pS, b_pS, k_bf, q_bf, qf_bf, qb_bf, b_qk,
                    mT, b_mT, DT, b_tab, pR, b_pR, v_sb, b_v, retT, b_ret):
        P = self.P
        bs = slice(blk * 128, (blk + 1) * 128)
        q = blk % 2
        P.op("act", lambda e: e.copy(out=Sbf[:], in_=S32[0][:]), reads=[b_S[0]], writes=[b_Sbf])

        def s_mm(hh):
            c, p0 = hh // 2, (hh % 2) * 64
            P.op("pe", lambda e, hh=hh, c=c, p0=p0: e.matmul(pS[hh % 2][:, :128], k_bf[p0:p0 + 64, c, bs], q_bf[p0:p0 + 64, c, bs], start=True, stop=True),
                 reads=[b_qk], writes=[b_pS[hh % 2]])

        s_mm(0)
        for hh in range(6):
            if hh + 1 < 6:
                s_mm(hh + 1)
            c, p0 = hh // 2, (hh % 2) * 64
            P.op("dve", lambda e, hh=hh: e.tensor_tensor(out=mT[hh % 2][:], in0=pS[hh % 2][:, :128], in1=DT[:, hh, :], op=ALU.mult),
                 reads=[b_pS[hh % 2], b_tab], writes=[b_mT[hh % 2]])
            P.op("pe", lambda e, hh=hh: e.matmul(pR[:, :128], v_sb[:, blk, hh * 128:(hh + 1) * 128], mT[hh % 2][:], start=True, stop=False),
                 reads=[b_v, b_mT[hh % 2]], writes=[b_pR], inc=False)
            P.op("pe", lambda e, hh=hh, c=c, p0=p0: e.matmul(pR[:, :128], Sbf[p0:p0 + 64, c, :], qf_bf[p0:p0 + 64, c, bs], start=False, stop=False),
                 reads=[b_Sbf, b_qk], writes=[], inc=False)
            P.op("pe", lambda e, hh=hh, c=c, p0=p0, q=q: e.matmul(pR[:, :128], sbt[q][p0:p0 + 64, c, :], qb_bf[p0:p0 + 64, c, bs], start=False, stop=True),
                 reads=[b_sbt[q], b_qk], writes=[b_pR])
            P.op("act", lambda e, hh=hh: e.copy(out=retT[:, hh, bs], in_=pR[:, :128]), reads=[b_pR], writes=[b_ret])


    def declare_mla_inputs(self, di):
        T, TT = self.T, self.TT
        nc = self.nc
        self.mla_w_in = di("mla_w_in", [2, D, 672])
        self.mla_w_krsw = di("mla_w_krsw", [2, D, 96])
        self.mla_w_qb = di("mla_w_qb", [2, 384, 768])
        self.mla_w_qbsw = di("mla_w_qbsw", [2, 384, 768])
        self.mla_w_kvb = di("mla_w_kvb", [2, 256, 1536])
        self.mla_w_out = di("mla_w_out", [2, D, D])
        self.mla_g_in = di("mla_g", [128, 2, 9])
        self.rope_m_in = di("rope_m", [2, 96, T])
        self.qT_scr = nc.dram_tensor("qT_scr", [8, 96, TT], BF16, kind="Internal").ap()
        self.kT_scr = nc.dram_tensor("kT_scr", [8, 96, TT], BF16, kind="Internal").ap()
        self.v_scr = nc.dram_tensor("v_scr", [TT, 1024], BF16, kind="Internal").ap()
        self.oT_scr = nc.dram_tensor("oT_scr", [8, 128, TT], BF16, kind="Internal").ap()
        self.b_qscr, self.b_kscr, self.b_vscr, self.b_oscr = Buf("qscr"), Buf("kscr"), Buf("vscr"), Buf("oscr")

    def phase_mla(self, sub, i, src="scratch", dst="scratch"):
        if sub == 1:
            self.phase_mla1(i, src)
        elif sub == 2:
            self.phase_mla2(i)
        else:
            self.phase_mla3(i, src, dst)

    def phase_mla1(self, i, src):
        nc, P, sb, ps = self.nc, self.P, self.sb, self.ps
        jl = i // 2
        last = i == DEPTH - 1
        T = self.T
        w_mi = sb("w_mi", [128, NCH, 672 + 96], BF16)
        b_w = Buf("w")
        P.dma("pool", w_mi[:, :, 0:672], self.mla_w_in[jl].rearrange("(kc p) n -> p kc n", p=128), writes=[b_w])
        P.dma("pool", w_mi[:, :, 672:768], self.mla_w_krsw[jl].rearrange("(kc p) n -> p kc n", p=128), writes=[b_w])
        w_qb = sb("w_qb", [128, 3, 2, 768], BF16)
        P.dma("pool", w_qb[:, :, 0, :], self.mla_w_qb[jl].rearrange("(kc p) n -> p kc n", p=128), writes=[b_w])
        P.dma("pool", w_qb[:, :, 1, :], self.mla_w_qbsw[jl].rearrange("(kc p) n -> p kc n", p=128), writes=[b_w])
        w_kvb = sb("w_kvb", [128, 2, 8, 192], BF16)
        P.dma("pool", w_kvb[:].rearrange("p a b c -> p a (b c)"), self.mla_w_kvb[jl].rearrange("(kc p) n -> p kc n", p=128), writes=[b_w])
        gm = sb("gm", [128, 9], F32)
        P.dma("sp", gm[:], self.mla_g_in[:, jl, :], writes=[b_w])
        xts = [sb(f"xt{q}", [128, NCH, NT], F32) for q in range(2)]
        b_xts = [Buf("xt0"), Buf("xt1")]
        Wn = self.norm_work()
        sq, b_sq, pss, b_pss, rt, b_rt, rstd, b_rstd = Wn[:8]
        h, b_h = sb("h", [128, NCH, NT], BF16), Buf("h")
        ppA, b_ppA = ps("ppA", [128, 512], F32), Buf("ppA")
        pq4, b_pq4 = ps("pq4", [128, 4, NT], F32), Buf("pq4")
        pqs4, b_pqs4 = ps("pqs4", [128, 4, NT], F32), Buf("pqs4")
        pn4, b_pn4 = ps("pn4", [128, 4, NT], F32), Buf("pn4")
        pst, b_pst = ppA, b_ppA
        xin, b_xin = (sb("xin", [128, 2, D], F32), Buf("xin")) if src == "input" else (None, None)
        qa32, b_qa32 = sb("qa32", [128, 5, NT], F32), Buf("qa32")
        qan, b_qan = sb("qan", [128, 5, NT], BF16), Buf("qan")
        rope, b_rope = sb("rope", [96, 2, NT], F32), Buf("rope")
        krr, b_krr = sb("krr", [96, NT], F32), Buf("krr")
        krsq, b_krsq = sb("krsq", [96, NT], BF16), Buf("krsq")
        krss, b_krss = sb("krss", [96, NT], F32), Buf("krss")
        t2, b_t2 = sb("t2", [96, 4, NT], F32), Buf("t2")
        t3, b_t3 = sb("t3", [96, 4, NT], F32), Buf("t3")
        s96, b_s96 = sb("s96", [96, 4, NT], BF16), Buf("s96")
        rq, b_rq = sb("rq", [96, 4, NT], F32), Buf("rq")
        qo = [sb(f"qo{q}", [96, 4, NT], BF16) for q in range(2)]
        b_qo = [Buf("qo0"), Buf("qo1")]
        ko = [sb(f"ko{q}", [96, 4, NT], BF16) for q in range(2)]
        b_ko = [Buf("ko0"), Buf("ko1")]
        vt = [sb(f"vt{q}", [128, 1024], BF16) for q in range(2)]
        b_vt = [Buf("vt0"), Buf("vt1")]
        tl = slice(64, 96)

        def group_norm(nchunks, c0, gcol0, dim, n):
            P.op("act", lambda e: e.activation(out=sq[:, c0:c0 + nchunks, :n], in_=qa32[:, c0:c0 + nchunks, :n], func=AF.Square),
                 reads=[b_qa32], writes=[b_sq])
            for c in range(nchunks):
                P.op("pe", lambda e, c=c: e.matmul(pss[:, :n], self.ones_bf[:], sq[:, c0 + c, :n], start=(c == 0), stop=(c == nchunks - 1)),
                     reads=[b_sq, self.b_const], writes=[b_pss] if c in (0, nchunks - 1) else [], inc=(c == nchunks - 1))
            P.op("act", lambda e: e.activation(out=rt[:, :n], in_=pss[:, :n], func=AF.Sqrt, bias=self.eps_col[:, 0:1], scale=1.0 / dim),
                 reads=[b_pss, self.b_const], writes=[b_rt])
            P.op("dve", lambda e: e.reciprocal(out=rstd[:, :n], in_=rt[:, :n]), reads=[b_rt], writes=[b_rstd])
            for c in range(nchunks):
                P.op("dve", lambda e, c=c: e.scalar_tensor_tensor(out=qan[:, c0 + c, :n], in0=qa32[:, c0 + c, :n], scalar=gm[:, gcol0 + c:gcol0 + c + 1],
                                                                 in1=rstd[:, :n], op0=ALU.mult, op1=ALU.mult),
                     reads=[b_qa32, b_rstd, b_w], writes=[b_qan])

        def bc4(ap2d, rows, n):
            return ap2d.unsqueeze(1).broadcast_to([rows, 4, n])

        qans = [qan, sb("qan1", [128, 5, NT], BF16)]
        b_qans = [b_qan, Buf("qan1")]
        ropes = [rope, sb("rope1", [96, 2, NT], F32)]
        b_ropes = [b_rope, Buf("rope1")]
        krrs = [krr, sb("krr1", [96, NT], F32)]
        b_krrs = [b_krr, Buf("krr1")]
        krsss = [krss, sb("krss1", [96, NT], F32)]
        b_krsss = [b_krss, Buf("krss1")]
        t2a, b_t2a = sb("t2a", [96, NT], F32), Buf("t2a")

        def stageA(ti, slot, xt, b_xt):
            col0, n, r = self.tiles[ti]
            qan_, b_qan_ = qans[slot], b_qans[slot]
            rope_, b_rope_ = ropes[slot], b_ropes[slot]
            krr_, b_krr_ = krrs[slot], b_krrs[slot]
            krss_, b_krss_ = krsss[slot], b_krsss[slot]
            steps = list(self.norm_steps(xt, b_xt, n, i, 1, r, h, b_h, Wn))
            if r == 0:
                steps.append(lambda: P.dma("sp", rope_[:, :, :n], self.rope_m_in[:, :, col0:col0 + n].rearrange("a p t -> p a t"), writes=[b_rope_]))

            def proj(c):
                for kc in range(NCH):
                    P.op("pe", lambda e, kc=kc, c=c: e.matmul(ppA[:, :n], w_mi[:, kc, c * 128:(c + 1) * 128], h[:, kc, :n],
                                                             start=(kc == 0), stop=(kc == NCH - 1)),
                         reads=[b_h, b_w], writes=[b_ppA] if kc in (0, NCH - 1) else [], inc=(kc == NCH - 1))
                P.op("act", lambda e, c=c: e.copy(out=qa32[:, c, :n], in_=ppA[:, :n]), reads=[b_ppA], writes=[b_qa32])
            for c in range(5):
                steps.append(lambda c=c: proj(c))

            def gnorm(nchunks, c0, gcol0, dim):
                P.op("act", lambda e: e.activation(out=sq[:, c0:c0 + nchunks, :n], in_=qa32[:, c0:c0 + nchunks, :n], func=AF.Square),
                     reads=[b_qa32], writes=[b_sq])
                for c in range(nchunks):
                    P.op("pe", lambda e, c=c: e.matmul(pss[:, :n], self.ones_bf[:], sq[:, c0 + c, :n], start=(c == 0), stop=(c == nchunks - 1)),
                         reads=[b_sq, self.b_const], writes=[b_pss] if c in (0, nchunks - 1) else [], inc=(c == nchunks - 1))
                P.op("act", lambda e: e.activation(out=rt[:, :n], in_=pss[:, :n], func=AF.Sqrt, bias=self.eps_col[:, 0:1], scale=1.0 / dim),
                     reads=[b_pss, self.b_const], writes=[b_rt])
                P.op("dve", lambda e: e.reciprocal(out=rstd[:, :n], in_=rt[:, :n]), reads=[b_rt], writes=[b_rstd])
                for c in range(nchunks):
                    P.op("dve", lambda e, c=c: e.scalar_tensor_tensor(out=qan_[:, c0 + c, :n], in0=qa32[:, c0 + c, :n], scalar=gm[:, gcol0 + c:gcol0 + c + 1],
                                                                     in1=rstd[:, :n], op0=ALU.mult, op1=ALU.mult),
                         reads=[b_qa32, b_rstd, b_w], writes=[b_qan_])
            steps.append(lambda: gnorm(3, 0, 0, 384))
            steps.append(lambda: gnorm(2, 3, 3, 256))

            def kr1():
                for kc in range(NCH):
                    P.op("pe", lambda e, kc=kc: e.matmul(ppA[:96, :n], w_mi[:, kc, 576:672], h[:, kc, :n], start=(kc == 0), stop=(kc == NCH - 1)),
                         reads=[b_h, b_w], writes=[b_ppA] if kc in (0, NCH - 1) else [], inc=(kc == NCH - 1))
                P.op("act", lambda e: e.activation(out=krsq[tl, :n], in_=ppA[tl, :n], func=AF.Square), reads=[b_ppA], writes=[b_krsq])
                if r == 0:
                    P.op("dve", lambda e: e.scalar_tensor_tensor(out=krr_[tl, :n], in0=ppA[tl, :n], scalar=gm[tl, 7:8], in1=rope_[tl, 0, :n], op0=ALU.mult, op1=ALU.mult),
                         reads=[b_ppA, b_rope_, b_w], writes=[b_krr_])
                else:
                    P.op("dve", lambda e: e.tensor_scalar(krr_[tl, :n], ppA[tl, :n], gm[tl, 7:8], None, ALU.mult), reads=[b_ppA, b_w], writes=[b_krr_])

            def kr2():
                if r == 0:
                    for kc in range(NCH):
                        P.op("pe", lambda e, kc=kc: e.matmul(ppA[:96, :n], w_mi[:, kc, 672:768], h[:, kc, :n], start=(kc == 0), stop=(kc == NCH - 1)),
                             reads=[b_h, b_w], writes=[b_ppA] if kc in (0, NCH - 1) else [], inc=(kc == NCH - 1))
                    P.op("dve", lambda e: e.scalar_tensor_tensor(out=t2a[tl, :n], in0=ppA[tl, :n], scalar=gm[tl, 8:9], in1=rope_[tl, 1, :n], op0=ALU.mult, op1=ALU.mult),
                         reads=[b_ppA, b_rope_, b_w], writes=[b_t2a])
                    P.op("dve", lambda e: e.tensor_tensor(out=krr_[tl, :n], in0=krr_[tl, :n], in1=t2a[tl, :n], op=ALU.add), reads=[b_t2a, b_krr_], writes=[b_krr_])
                P.op("pe", lambda e: e.matmul(ppA[:96, :n], self.ones_bf[64:96, 0:96], krsq[64:96, :n], start=True, stop=True),
                     reads=[b_krsq, self.b_const], writes=[b_ppA])
                P.op("act", lambda e: e.copy(out=krss_[:, :n], in_=ppA[:96, :n]), reads=[b_ppA], writes=[b_krss_])
            steps.append(kr1)
            steps.append(kr2)
            return steps

        def stageB(ti, slot):
            col0, n, r = self.tiles[ti]
            qan_, b_qan_ = qans[slot], b_qans[slot]
            rope_, b_rope_ = ropes[slot], b_ropes[slot]
            krr_, b_krr_ = krrs[slot], b_krrs[slot]
            krss_, b_krss_ = krsss[slot], b_krsss[slot]
            need_q = not (last and r == 1)
            steps = []

            def q1(g4):
                for a in range(4):
                    hh = 4 * g4 + a
                    for kc in range(3):
                        P.op("pe", lambda e, kc=kc, hh=hh, a=a: e.matmul(pq4[:96, a, :n], w_qb[:, kc, 0, hh * 96:(hh + 1) * 96], qan_[:, kc, :n], start=(kc == 0), stop=(kc == 2)),
                             reads=[b_qan_, b_w], writes=[b_pq4] if kc in (0, 2) else [], inc=(kc == 2))
                P.op("act", lambda e: e.activation(out=s96[:, :, :n], in_=pq4[:96, :, :n], func=AF.Square), reads=[b_pq4], writes=[b_s96])
                if r == 0:
                    for a in range(4):
                        hh = 4 * g4 + a
                        for kc in range(3):
                            P.op("pe", lambda e, kc=kc, hh=hh, a=a: e.matmul(pqs4[:96, a, :n], w_qb[:, kc, 1, hh * 96:(hh + 1) * 96], qan_[:, kc, :n], start=(kc == 0), stop=(kc == 2)),
                                 reads=[b_qan_, b_w], writes=[b_pqs4] if kc in (0, 2) else [], inc=(kc == 2))

            def q2(g4):
                for a in range(4):
                    P.op("pe", lambda e, a=a: e.matmul(pn4[:96, a, :n], self.ones_bf[0:96, 0:96], s96[:, a, :n], start=True, stop=True),
                         reads=[b_s96, self.b_const], writes=[b_pn4], inc=(a == 3))
                P.op("act", lambda e: e.activation(out=rq[:, :, :n], in_=pn4[:96, :, :n], func=AF.Sqrt, bias=self.eps_col[0:96, 0:1], scale=1.0 / 96),
                     reads=[b_pn4, self.b_const], writes=[b_rq])
                P.op("dve", lambda e: e.reciprocal(out=rq[:, :, :n], in_=rq[:, :, :n]), reads=[b_rq], writes=[b_rq])

            def q3(g4):
                qb_, bq_ = qo[g4 % 2], b_qo[g4 % 2]
                if r == 0:
                    P.op("dve", lambda e: e.scalar_tensor_tensor(out=qb_[0:64, :, :n], in0=pq4[0:64, :, :n], scalar=gm[0:64, 5:6], in1=rq[0:64, :, :n], op0=ALU.mult, op1=ALU.mult),
                         reads=[b_pq4, b_rq, b_w], writes=[bq_])
                    P.op("dve", lambda e: e.scalar_tensor_tensor(out=t3[tl, :, :n], in0=pq4[tl, :, :n], scalar=gm[tl, 5:6], in1=bc4(rope_[tl, 0, :n], 32, n), op0=ALU.mult, op1=ALU.mult),
                         reads=[b_pq4, b_rope_, b_w], writes=[b_t3])
                    P.op("dve", lambda e: e.scalar_tensor_tensor(out=t2[tl, :, :n], in0=pqs4[tl, :, :n], scalar=gm[tl, 6:7], in1=bc4(rope_[tl, 1, :n], 32, n), op0=ALU.mult, op1=ALU.mult),
                         reads=[b_pqs4, b_rope_, b_w], writes=[b_t2])
                    P.op("dve", lambda e: e.tensor_tensor(out=t3[tl, :, :n], in0=t3[tl, :, :n], in1=t2[tl, :, :n], op=ALU.add), reads=[b_t2, b_t3], writes=[b_t3])
                    P.op("dve", lambda e: e.tensor_tensor(out=qb_[tl, :, :n], in0=t3[tl, :, :n], in1=rq[tl, :, :n], op=ALU.mult), reads=[b_t3, b_rq], writes=[bq_])
                else:
                    P.op("dve", lambda e: e.scalar_tensor_tensor(out=qb_[:, :, :n], in0=pq4[:96, :, :n], scalar=gm[0:96, 5:6], in1=rq[:, :, :n], op0=ALU.mult, op1=ALU.mult),
                         reads=[b_pq4, b_rq, b_w], writes=[bq_])
                P.dma("sp", self.qT_scr[4 * g4:4 * g4 + 4, :, col0:col0 + n].rearrange("h p t -> p h t"), qb_[:, :, :n], reads=[bq_], writes=[self.b_qscr])

            def k1(g4):
                for a in range(4):
                    hh = 4 * g4 + a
                    for kc in range(2):
                        P.op("pe", lambda e, kc=kc, hh=hh, a=a: e.matmul(pq4[:64, a, :n], w_kvb[:, kc, hh, 0:64], qan_[:, 3 + kc, :n], start=(kc == 0), stop=(kc == 1)),
                             reads=[b_qan_, b_w], writes=[b_pq4], inc=(kc == 1))
                P.op("act", lambda e: e.activation(out=s96[0:64, :, :n], in_=pq4[:64, :, :n], func=AF.Square), reads=[b_pq4], writes=[b_s96])

            def k2(g4):
                for a in range(4):
                    P.op("pe", lambda e, a=a: e.matmul(pn4[:96, a, :n], self.ones_bf[0:64, 0:96], s96[0:64, a, :n], start=True, stop=True),
                         reads=[b_s96, self.b_const], writes=[b_pn4], inc=(a == 3))
                P.op("dve", lambda e: e.tensor_tensor(out=rq[:, :, :n], in0=pn4[:96, :, :n], in1=bc4(krss_[:, :n], 96, n), op=ALU.add),
                     reads=[b_pn4, b_krss_], writes=[b_rq])
                P.op("act", lambda e: e.activation(out=rq[:, :, :n], in_=rq[:, :, :n], func=AF.Sqrt, bias=self.eps_col[0:96, 0:1], scale=1.0 / 96),
                     reads=[b_rq, self.b_const], writes=[b_rq])
                P.op("dve", lambda e: e.reciprocal(out=rq[:, :, :n], in_=rq[:, :, :n]), reads=[b_rq], writes=[b_rq])

            def k3(g4):
                kb_, bk_ = ko[g4 % 2], b_ko[g4 % 2]
                P.op("dve", lambda e: e.scalar_tensor_tensor(out=kb_[0:64, :, :n], in0=pq4[:64, :, :n], scalar=gm[0:64, 7:8], in1=rq[0:64, :, :n], op0=ALU.mult, op1=ALU.mult),
                     reads=[b_pq4, b_rq, b_w], writes=[bk_])
                P.op("dve", lambda e: e.tensor_tensor(out=kb_[tl, :, :n], in0=rq[tl, :, :n], in1=bc4(krr_[tl, :n], 32, n), op=ALU.mult), reads=[b_krr_, b_rq], writes=[bk_])
                P.dma("sp", self.kT_scr[4 * g4:4 * g4 + 4, :, col0:col0 + n].rearrange("h p t -> p h t"), kb_[:, :, :n], reads=[bk_], writes=[self.b_kscr])

            def vblk(blk):
                vb_, bv_ = vt[blk % 2], b_vt[blk % 2]
                for half in range(2):
                    pv, b_pv = (pqs4, b_pqs4) if half == 0 else (pn4, b_pn4)
                    for kc in range(2):
                        P.op("pe", lambda e, kc=kc, half=half, pv=pv: e.matmul(
                            pv[:, 0:2, :].rearrange("p a (b c) -> p (a b) c", b=2), qan_[:, 3 + kc, blk * 128:(blk + 1) * 128],
                            w_kvb[:, kc, 4 * half:4 * half + 4, 64:192], start=(kc == 0), stop=(kc == 1)),
                            reads=[b_qan_, b_w], writes=[b_pv], inc=(kc == 1))
                    P.op("act", lambda e, half=half, pv=pv: e.copy(out=vb_[:, half * 512:(half + 1) * 512], in_=pv[:, 0:2, :].rearrange("p a b -> p (a b)")),
                         reads=[b_pv], writes=[bv_])
                P.dma("sp", self.v_scr[col0 + blk * 128:col0 + (blk + 1) * 128, :], vb_[:, :], reads=[bv_], writes=[self.b_vscr])

            for g4 in range(2):
                if need_q:
                    steps += [lambda g4=g4: q1(g4), lambda g4=g4: q2(g4), lambda g4=g4: q3(g4)]
                steps += [lambda g4=g4: k1(g4), lambda g4=g4: k2(g4), lambda g4=g4: k3(g4)]
            for blk in range(n // 128):
                steps.append(lambda blk=blk: vblk(blk))
            return steps

        nt_ = len(self.tiles)
        self.load_x_tile(0, xts[0], b_xts[0], src, xin, b_xin, pst, b_pst)
        if nt_ > 1:
            self.load_x_tile(1, xts[1], b_xts[1], src, xin, b_xin, pst, b_pst)
        for st in stageA(0, 0, xts[0], b_xts[0]):
            st()
        for ti in range(nt_):
            sB = stageB(ti, ti % 2)
            sA = stageA(ti + 1, (ti + 1) % 2, xts[(ti + 1) % 2], b_xts[(ti + 1) % 2]) if ti + 1 < nt_ else []
            ia = ib = 0
            while ia < len(sA) or ib < len(sB):
                if ib < len(sB):
                    sB[ib]()
                    ib += 1
                for _ in range(2):
                    if ia < len(sA):
                        sA[ia]()
                        ia += 1
            if ti + 2 < nt_:
                self.load_x_tile(ti + 2, xts[ti % 2], b_xts[ti % 2], src, xin, b_xin, pst, b_pst)

    def phase_mla2(self, i):
        nc, P, sb, ps = self.nc, self.P, self.sb, self.ps
        last = i == DEPTH - 1
        T, TT = self.T, self.TT
        NKT = TT // 128
        Kh = [sb(f"Kh{q}", [96, TT], BF16) for q in range(2)]
        Qh = [sb(f"Qh{q}", [96, TT], BF16) for q in range(2)]
        Vh = [sb(f"Vh{q}", [128, NKT, 128], BF16) for q in range(2)]
        b_K, b_Q, b_V = [Buf("K0"), Buf("K1")], [Buf("Q0"), Buf("Q1")], [Buf("V0"), Buf("V1")]
        pS = [ps(f"pS{q}", [128, 2, 512], F32) for q in range(2)]
        b_pS = [Buf(f"pS{q}") for q in range(2)]
        po = [ps(f"po{q}", [128, 512], F32) for q in range(2)]
        b_po = [Buf("po0"), Buf("po1")]
        pd = [ps(f"pd{q}", [128, 512], F32) for q in range(2)]
        b_pd = [Buf("pd0"), Buf("pd1")]
        Pt = [sb(f"Pt{q}", [128, 2, 512], BF16) for q in range(3)]
        b_Pt = [Buf(f"Pt{q}") for q in range(3)]
        accd = [sb(f"accd{q}", [128, 512], F32) for q in range(2)]
        b_accd = [Buf("accd0"), Buf("accd1")]
        rd = [sb(f"rd{q}", [128, 512], F32) for q in range(2)]
        b_rd = [Buf("rd0"), Buf("rd1")]
        osb = [sb(f"osb{q}", [128, 512], BF16) for q in range(2)]
        b_osb = [Buf("osb0"), Buf("osb1")]
        ones32 = sb("ones32", [128, 128], F32)
        b_o32 = Buf("ones32")
        P.op("dve", lambda e: e.memset(ones32[:], 1.0), writes=[b_o32])
        scale = float(96 ** -0.5)
        NQ = min(512, T)
        qtiles = [(q0, NQ, list(range(NKT))) for q0 in range(0, T, NQ)]
        if not last:
            qtiles.append((T, CTX, list(range(T // 128, NKT))))
        cnt = [0]

        def attend(hh, hb, q0, nq, kts, qi):
            o_, d_ = po[qi % 2], pd[qi % 2]
            npair = len(kts) // 2
            assert len(kts) % 2 == 0
            ad, b_ad = accd[qi % 2], b_accd[qi % 2]

            def s_mm(idx):
                g = cnt[0] + idx
                for w in range(2):
                    kt = kts[2 * idx + w]
                    P.op("pe", lambda e, kt=kt, g=g, w=w: e.matmul(pS[g % 2][:, w, :nq], Kh[hb][:, kt * 128:(kt + 1) * 128], Qh[hb][:, q0:q0 + nq], start=True, stop=True),
                         reads=[b_K[hb], b_Q[hb]], writes=[b_pS[g % 2]], inc=(w == 1))

            s_mm(0)
            for idx in range(npair):
                g = cnt[0] + idx
                if idx + 1 < npair:
                    s_mm(idx + 1)
                P.op("act", lambda e, g=g: e.activation(out=Pt[g % 3][:, :, :nq], in_=pS[g % 2][:, :, :nq], func=AF.Exp, scale=scale),
                     reads=[b_pS[g % 2]], writes=[b_Pt[g % 3]])
                for w in range(2):
                    kt = kts[2 * idx + w]
                    first, lst = (idx == 0 and w == 0), (idx == npair - 1 and w == 1)
                    P.op("pe", lambda e, kt=kt, g=g, w=w, first=first, lst=lst: e.matmul(o_[:, :nq], Vh[hb][:, kt, :], Pt[g % 3][:, w, :nq], start=first, stop=lst),
                         reads=[b_V[hb], b_Pt[g % 3]], writes=[b_po[qi % 2]] if (first or lst) else [], inc=(w == 1))
                if idx % 4 != 3:
                    P.op("pe", lambda e, g=g, idx=idx: e.matmul(d_[:, :nq], self.ones_bf[:], Pt[g % 3][:, 0, :nq], start=(idx == 0), stop=False),
                         reads=[b_Pt[g % 3], self.b_const], writes=[b_pd[qi % 2]] if idx == 0 else [])
                else:
                    P.op("dve", lambda e, g=g: e.tensor_tensor(out=ad[:, :nq], in0=ad[:, :nq], in1=Pt[g % 3][:, 0, :nq], op=ALU.add), reads=[b_Pt[g % 3], b_ad], writes=[b_ad])
                if idx == 0:
                    P.op("dve", lambda e, g=g: e.tensor_copy(out=ad[:, :nq], in_=Pt[g % 3][:, 1, :nq]), reads=[b_Pt[g % 3]], writes=[b_ad])
                else:
                    P.op("dve", lambda e, g=g: e.tensor_tensor(out=ad[:, :nq], in0=ad[:, :nq], in1=Pt[g % 3][:, 1, :nq], op=ALU.add), reads=[b_Pt[g % 3], b_ad], writes=[b_ad])
            cnt[0] += npair
            P.op("pe", lambda e: e.matmul(d_[:, :nq], ones32[:], ad[:, :nq], start=False, stop=True), reads=[b_ad, b_o32], writes=[b_pd[qi % 2]])
            P.op("dve", lambda e: e.reciprocal(out=rd[qi % 2][:, :nq], in_=d_[:, :nq]), reads=[b_pd[qi % 2]], writes=[b_rd[qi % 2]])
            P.op("dve", lambda e: e.tensor_tensor(out=osb[qi % 2][:, :nq], in0=o_[:, :nq], in1=rd[qi % 2][:, :nq], op=ALU.mult),
                 reads=[b_po[qi % 2], b_rd[qi % 2]], writes=[b_osb[qi % 2]])
            P.dma("sp", self.oT_scr[hh, :, q0:q0 + nq], osb[qi % 2][:, :nq], reads=[b_osb[qi % 2]], writes=[self.b_oscr])

        qi = 0
        TQ = T if last else TT

        def load_head(hh):
            hb = hh % 2
            P.dma("sp", Kh[hb][:, :], self.kT_scr[hh, :, :], reads=[self.b_kscr], writes=[b_K[hb]])
            P.dma("sp", Qh[hb][:, :TQ], self.qT_scr[hh, :, :TQ], reads=[self.b_qscr], writes=[b_Q[hb]])
            P.dma("sp", Vh[hb][:, :, :], self.v_scr[:, hh * 128:(hh + 1) * 128].rearrange("(kt p) e -> p kt e", p=128), reads=[self.b_vscr], writes=[b_V[hb]])

        load_head(0)
        for hh in range(8):
            hb = hh % 2
            if hh + 1 < 8:
                load_head(hh + 1)
            for (q0, nq, kts) in qtiles:
                attend(hh, hb, q0, nq, kts, qi)
                qi += 1

    def phase_mla3(self, i, src, dst):
        nc, P, sb, ps = self.nc, self.P, self.sb, self.ps
        jl = i // 2
        last = i == DEPTH - 1
        w_mo = sb("w_mo", [128, NCH, D], BF16)
        b_w = Buf("w")
        P.dma("pool", w_mo[:, :, :], self.mla_w_out[jl].rearrange("(kc p) n -> p kc n", p=128), writes=[b_w])
        xts = [sb(f"xt{q}", [128, NCH, NT], F32) for q in range(2)]
        b_xts = [Buf("xt0"), Buf("xt1")]
        ots = [sb(f"ot{q}", [128, 8, NT], BF16) for q in range(2)]
        b_ots = [Buf("ot0"), Buf("ot1")]
        pp = [ps(f"pp{q}", [128, 512], F32) for q in range(2)]
        b_pp = [Buf("pp0"), Buf("pp1")]
        pst, b_pst = ps("pst", [128, 512], F32), Buf("pst")
        xin, b_xin = (sb("xin", [128, 2, D], F32), Buf("xin")) if src == "input" else (None, None)
        oo, b_oo = (sb("oo", [128, 2, D], F32), Buf("oo")) if dst == "output" else (None, None)
        k = [0]

        def load3(ti):
            col0, n, r = self.tiles[ti]
            self.load_x_tile(ti, xts[ti % 2], b_xts[ti % 2], src, xin, b_xin, pst, b_pst)
            P.dma("sp", ots[ti % 2][:, :, :n], self.oT_scr[:, :, col0:col0 + n].rearrange("h p t -> p h t"), reads=[self.b_oscr], writes=[b_ots[ti % 2]])

        def do_tile(ti, xt, b_xt, ot, b_ot):
            col0, n, r = self.tiles[ti]
            for m in range(NCH):
                pt, b_pt = pp[k[0] % 2], b_pp[k[0] % 2]
                k[0] += 1
                for hh in range(8):
                    P.op("pe", lambda e, m=m, hh=hh, pt=pt: e.matmul(pt[:, :n], w_mo[:, hh, m * 128:(m + 1) * 128], ot[:, hh, :n], start=(hh == 0), stop=(hh == 7)),
                         reads=[b_ot, b_w], writes=[b_pt] if hh in (0, 7) else [], inc=(hh == 7))
                P.op("dve", lambda e, m=m, pt=pt: e.scalar_tensor_tensor(out=xt[:, m, :n], in0=pt[:, :n], scalar=self.modG[:, i, 1, m, r:r + 1], in1=xt[:, m, :n],
                                                                        op0=ALU.mult, op1=ALU.add), reads=[b_pt, b_xt, self.b_mod], writes=[b_xt])
            self.store_x_tile(ti, xt, b_xt, dst, oo, b_oo, [pst, pst], [b_pst, b_pst])

        tl3 = [ti for ti in range(len(self.tiles)) if not (last and self.tiles[ti][2] == 1)]
        load3(tl3[0])
        for q_, ti in enumerate(tl3):
            if q_ + 1 < len(tl3):
                load3(tl3[q_ + 1])
            do_tile(ti, xts[ti % 2], b_xts[ti % 2], ots[ti % 2], b_ots[ti % 2])


def full_plan():
    plan = [("mods", [0, 1, 2, 3])]
    for i in range(DEPTH):
        last = i == DEPTH - 1
        plan.append(("ffn", i, 0, "input" if i == 0 else "scratch", "scratch"))
        if i % 2 == 0:
            plan.append(("even", i, "scratch", "scratch"))
        else:
            plan.append(("mla", 1, i, "scratch"))
            plan.append(("mla", 2, i))
            plan.append(("mla", 3, i, "scratch", "scratch"))
        plan.append(("ffn", i, 1, "scratch", "output" if last else "scratch", last))
    return plan


GRID_W = 64


def rope_tables(T, rot_dim, nrows_rep):
    n_freq = rot_dim // 4
    inv = (np.float32(10000.0) ** (-np.arange(n_freq, dtype=np.float32) / np.float32(n_freq))).astype(np.float32)
    t = np.arange(T)
    row = (t // GRID_W).astype(np.float32)
    col = (t % GRID_W).astype(np.float32)
    ang = np.concatenate([row[:, None] * inv, col[:, None] * inv], axis=-1).astype(np.float32)
    cos = np.cos(ang).astype(np.float32)
    sin = np.sin(ang).astype(np.float32)
    cos_f = np.repeat(cos, 2, axis=1).T
    sin_f = np.repeat(sin, 2, axis=1).T
    sign = np.where(np.arange(rot_dim) % 2 == 0, -1.0, 1.0).astype(np.float32)[:, None]
    sin_f = sin_f * sign
    return np.ascontiguousarray(np.stack([np.tile(cos_f, (nrows_rep, 1)), np.tile(sin_f, (nrows_rep, 1))], axis=0))


def even_host_inputs(inputs, T):
    f = lambda a: np.ascontiguousarray(np.asarray(a, dtype=np.float32))
    ab_w_in = f(inputs["ab_w_in"])
    qk = ab_w_in[:, :, 256:1024]
    ab_w_sw = np.ascontiguousarray(qk.reshape(2, D, 384, 2)[..., ::-1].reshape(2, D, 768))
    ld = f(inputs["ret_log_decay"])
    lgcol = np.zeros((128, 2, 2, 3), np.float32)
    for pr in range(3):
        lgcol[:64, :, :, pr] = ld[:, :, 2 * pr][None]
        lgcol[64:, :, :, pr] = ld[:, :, 2 * pr + 1][None]
    lgbc = np.ascontiguousarray(np.broadcast_to(ld.reshape(1, 24), (128, 24)))
    jj = np.arange(128, dtype=np.float32)[:, None]
    ii = np.arange(128, dtype=np.float32)[None, :]
    one = np.ones((128, 1), np.float32)
    rtab = np.stack([np.maximum(ii - jj, 0), np.maximum(jj - ii, 0), (ii >= jj).astype(np.float32), (jj >= ii).astype(np.float32),
                     one * (ii + 1), one * (128 - ii), one * (127 - ii), one * ii], axis=0).astype(np.float32)
    invc = np.zeros((2, 128, T + CTX), np.float32)
    for g, w in enumerate((2, 4, 8, 16)):
        lo = w // 2
        hi = w - 1 - lo
        for (L, c0) in ((T, 0), (CTX, T)):
            t = np.arange(L)
            cnt = (np.minimum(t + hi + 1, L) - np.maximum(t - lo, 0)).astype(np.float32)
            invc[g // 2, (g % 2) * 64:(g % 2) * 64 + 64, c0:c0 + L] = (np.float32(1.0) / cnt)[None, :]
    return {
        "ab_w_in": ab_w_in, "ab_w_sw": ab_w_sw, "ab_w_out": f(inputs["ab_w_out"]), "pool_w": f(inputs["pool_w"]),
        "pool_scale": np.ascontiguousarray(f(inputs["pool_scale"]).reshape(2, 2, 128).transpose(2, 0, 1).reshape(128, 4)),
        "ret_norm_g": np.ascontiguousarray(f(inputs["ret_norm_g"]).reshape(2, 6, 128).transpose(2, 0, 1).reshape(128, 12)),
        "lgcol": lgcol, "lgbc": lgbc, "rtab": rtab, "rope_r": rope_tables(T, 64, 2), "invc": invc,
    }


def _swap_pairs(a):
    sh = a.shape
    return np.ascontiguousarray(a.reshape(sh[:-1] + (sh[-1] // 2, 2))[..., ::-1].reshape(sh))


def mla_host_inputs(inputs, T):
    f = lambda a: np.ascontiguousarray(np.asarray(a, dtype=np.float32))
    w_in = f(inputs["mla_w_in"])
    krsw = np.concatenate([w_in[:, :, 576:640], _swap_pairs(w_in[:, :, 640:672])], axis=-1)
    w_qb = f(inputs["mla_w_qb"])
    qbh = w_qb.reshape(2, 384, 8, 96)
    qbsw = np.concatenate([qbh[..., :64], _swap_pairs(qbh[..., 64:])], axis=-1).reshape(2, 384, 768)
    g = np.zeros((128, 2, 9), np.float32)
    qa_g, kva_g, qn_g, kn_g = f(inputs["mla_qa_g"]), f(inputs["mla_kva_g"]), f(inputs["mla_qn_g"]), f(inputs["mla_kn_g"])
    for jl in range(2):
        g[:, jl, 0:3] = qa_g[jl].reshape(3, 128).T
        g[:, jl, 3:5] = kva_g[jl].reshape(2, 128).T
        g[:96, jl, 5] = qn_g[jl]
        g[64:96, jl, 6] = _swap_pairs(qn_g[jl][64:])
        g[:96, jl, 7] = kn_g[jl]
        g[64:96, jl, 8] = _swap_pairs(kn_g[jl][64:])
    rm = np.zeros((2, 96, T), np.float32)
    rm[:, 64:96, :] = rope_tables(T, 32, 1)
    return {"mla_w_in": w_in, "mla_w_krsw": np.ascontiguousarray(krsw), "mla_w_qb": w_qb, "mla_w_qbsw": np.ascontiguousarray(qbsw),
            "mla_w_kvb": f(inputs["mla_w_kvb"]), "mla_w_out": f(inputs["mla_w_out"]), "mla_g": g, "rope_m": rm}


def make_in_maps(inputs, T, n_cores=8):
    f = lambda a: np.ascontiguousarray(np.asarray(a, dtype=np.float32))
    x, c, ctx, c_ctx = f(inputs["x"]), f(inputs["c"]), f(inputs["ctx"]), f(inputs["c_ctx"])
    B = x.shape[0]
    ada_b = f(inputs["ada_b"]).reshape(DEPTH, 9, NCH, 128).transpose(0, 3, 1, 2).reshape(DEPTH, 128, 72)
    norm_g = f(inputs["norm_g"]).reshape(DEPTH * 3, NCH, 128).transpose(2, 0, 1).reshape(128, DEPTH * 3 * NCH)
    shared = {
        "ident": np.eye(128, dtype=np.float32),
        "ada_w": f(inputs["ada_w"]), "ada_b": np.ascontiguousarray(ada_b), "norm_g": np.ascontiguousarray(norm_g),
        "ffn_w_in": f(inputs["ffn_w_in"]), "ffn_w_out": f(inputs["ffn_w_out"]),
    }
    shared.update(even_host_inputs(inputs, T))
    shared.update(mla_host_inputs(inputs, T))
    maps = []
    for core in range(n_cores):
        b = core % B
        cT = np.stack([c[b].reshape(NCH, 128).T, c_ctx.reshape(NCH, 128).T], axis=-1)
        m = dict(shared)
        m.update({"x": np.ascontiguousarray(x[b, :T]), "ctx": np.ascontiguousarray(ctx[b]), "cT": np.ascontiguousarray(cT)})
        maps.append(m)
    return maps


_NC_CACHE = {}


def run_plan(inputs, T, plan, n_cores=8):
    key = (T, repr(plan))
    if key not in _NC_CACHE:
        _NC_CACHE[key] = Builder(T, plan).build()
    nc = _NC_CACHE[key]
    maps = make_in_maps(inputs, T, n_cores)
    res = run_bass_kernel_spmd(nc, maps, core_ids=list(range(n_cores)))
    return res


N_CORES = 4


def kernel(**inputs):
    x = np.asarray(inputs["x"])
    B, T, _ = x.shape
    res = run_plan(inputs, T, full_plan(), n_cores=N_CORES)
    out = np.stack([np.asarray(res.results[b]["out"]) for b in range(B)], axis=0)
    return out.astype(np.float32)
```

```python
import numpy as np
from contextlib import ExitStack
import concourse.bass as bass
import concourse.mybir as mybir
from concourse.bass_utils import run_bass_kernel_spmd

F32 = mybir.dt.float32
BF16 = mybir.dt.bfloat16
AF = mybir.ActivationFunctionType
ALU = mybir.AluOpType
AX = mybir.AxisListType

D = 1024
NCH = 8
DFF = 2816
NJ = 22
CTX = 256
NT = 256
EPS = 1e-6
DEPTH = 4


class Buf:
    __slots__ = ("w", "r", "name")

    def __init__(self, name=""):
        self.w = None
        self.r = {}
        self.name = name


ENGS = ("pe", "act", "dve", "pool", "sp")


class Prog:
    def __init__(self, nc, n_sp_lanes=10, n_pool_lanes=6, n_act_lanes=0, same_engine_sync=True):
        self.nc = nc
        self.q = {e: [] for e in ENGS}
        self.cnt = {e: 0 for e in ENGS}
        self.lanes = {"sp": [f"sp{i}" for i in range(n_sp_lanes)],
                      "pool": [f"pl{i}" for i in range(n_pool_lanes)],
                      "act": [f"al{i}" for i in range(n_act_lanes)]}
        self.lane_rr = {"sp": 0, "pool": 0, "act": 0}
        self.lane_val = {}
        for q in self.lanes.values():
            for l in q:
                self.lane_val[l] = 0
        self.same = same_engine_sync
        self.n_ops = 0

    def _deps(self, engine, reads, writes):
        waits = {}

        def need(k, v):
            if v > waits.get(k, 0):
                waits[k] = v

        for b in reads:
            if b.w is not None:
                need(*b.w)
        for b in writes:
            if b.w is not None:
                need(*b.w)
            for k, v in b.r.items():
                need(k, v)
        return waits

    def op(self, engine, fn, reads=(), writes=(), inc=True):
        waits = self._deps(engine, reads, writes)
        if engine == "pe" or not self.same:
            waits.pop(engine, None)
        if inc:
            self.cnt[engine] += 1
            c = self.cnt[engine]
        else:
            c = self.cnt[engine] + 1
        self.q[engine].append((waits, fn, engine if inc else None, 1))
        for b in writes:
            b.w = (engine, c)
            b.r = {}
        for b in reads:
            if b.r.get(engine, 0) < c:
                b.r[engine] = c
        self.n_ops += 1

    def dma(self, queue, out, in_, reads=(), writes=()):
        lanes = self.lanes[queue]
        lane = lanes[self.lane_rr[queue] % len(lanes)]
        self.lane_rr[queue] += 1
        waits = self._deps(queue, reads, writes)
        if not self.same:
            waits.pop(queue, None)
        prev = self.lane_val[lane]
        if prev > 0:
            waits[lane] = max(waits.get(lane, 0), prev)
        v = prev + 16
        self.lane_val[lane] = v
        self.q[queue].append((waits, lambda e, o=out, i=in_: e.dma_start(out=o, in_=i), lane, 16))
        for b in writes:
            b.w = (lane, v)
            b.r = {}
        for b in reads:
            if b.r.get(lane, 0) < v:
                b.r[lane] = v
        self.n_ops += 1

    def barrier(self):
        snap = {e: self.cnt[e] for e in ENGS if self.cnt[e] > 0}
        snap.update({l: v for l, v in self.lane_val.items() if v > 0})
        for e in ENGS:
            w = dict(snap)
            w.pop(e, None)
            self.q[e].append((w, None, None, 0))

    def run(self):
        nc = self.nc
        with ExitStack() as es:
            sems = {}
            for k in list(ENGS) + list(self.lane_val.keys()):
                sems[k] = es.enter_context(nc.semaphore("s_" + k))
            block = es.enter_context(nc.Block())

            def replay(ename):
                def body(eng):
                    known = {}
                    for waits, fn, inckey, incval in self.q[ename]:
                        for k, v in waits.items():
                            if known.get(k, 0) < v:
                                eng.wait_ge(sems[k], v)
                                known[k] = v
                        if fn is None:
                            continue
                        ins = fn(eng)
                        if inckey is not None:
                            ins.then_inc(sems[inckey], incval)
                    for l in self.lanes.get(ename, []):
                        if self.lane_val[l] > 0 and known.get(l, 0) < self.lane_val[l]:
                            eng.wait_ge(sems[l], self.lane_val[l])
                return body

            block.tensor(replay("pe"))
            block.scalar(replay("act"))
            block.vector(replay("dve"))
            block.gpsimd(replay("pool"))
            block.sync(replay("sp"))


class Builder:
    def __init__(self, T, plan, debug=False):
        self.debug = debug
        self.T = T
        self.TT = T + CTX
        self.plan = plan
        nc = bass.Bass("TRN2", target_bir_lowering=False)
        self.nc = nc
        self.P = Prog(nc)
        di = lambda name, shape, dt=F32: nc.dram_tensor(name, list(shape), dt, kind="ExternalInput").ap()
        self.x_in = di("x", [T, D])
        self.ctx_in = di("ctx", [CTX, D])
        self.cT_in = di("cT", [128, NCH, 2])
        self.ident_in = di("ident", [128, 128])
        self.ada_w = di("ada_w", [DEPTH, D, 9 * D])
        self.ada_b = di("ada_b", [DEPTH, 128, 72])
        self.norm_g = di("norm_g", [128, DEPTH * 3 * NCH])
        self.ffn_w_in = di("ffn_w_in", [DEPTH, 2, D, 2 * DFF])
        self.ffn_w_out = di("ffn_w_out", [DEPTH, 2, DFF, D])
        self.out = nc.dram_tensor("out", [T, D], F32, kind="ExternalOutput").ap()
        self.xTs = [nc.dram_tensor(f"xT_scr{q}", [D, self.TT], F32, kind="Internal").ap() for q in range(2)]
        self.xcur = 0
        self.tiles = [(t * NT, NT, 0) for t in range(T // NT)] + [(T, CTX, 1)]
        self.xT_bufs = [[Buf(f"xT{q}_{t}") for t in range(len(self.tiles))] for q in range(2)]
        if debug:
            self.out_ctx = nc.dram_tensor("out_ctx", [CTX, D], F32, kind="ExternalOutput").ap()
        self.declare_mixer_inputs(di)
        self.declare_mla_inputs(di)

    def build(self):
        nc, P = self.nc, self.P
        with ExitStack() as es:
            sb = lambda name, shape, dt=F32: es.enter_context(nc.sbuf_tensor("g_" + name, list(shape), dt))
            self.ones_bf = sb("ones_bf", [128, 128], BF16)
            self.ident = sb("ident", [128, 128], F32)
            self.cT = sb("cT", [128, NCH, 2], F32)
            self.silu_c = sb("silu_c", [128, NCH, 32], BF16)
            self.g_sb = sb("g_sb", [128, DEPTH * 3 * NCH], F32)
            self.adab_sb = sb("adab_sb", [128, DEPTH, 72], F32)
            self.modT = sb("modT", [128, DEPTH, 72, 2], F32)
            self.modA = sb("modA", [128, DEPTH, 3, NCH, 2], F32)
            self.modG = sb("modG", [128, DEPTH, 3, NCH, 2], F32)
            self.eps_col = sb("eps_col", [128, 1], F32)
            self.ident_bf = sb("ident_bf", [128, 128], BF16)
            self.b_const = Buf("const")
            self.b_mod = Buf("mod")
            P.op("dve", lambda e: e.memset(self.eps_col[:], EPS), writes=[self.b_const])
            P.op("dve", lambda e: e.memset(self.ones_bf[:], 1.0), writes=[self.b_const])
            P.dma("sp", self.ident[:], self.ident_in[:, :], writes=[self.b_const])
            P.op("dve", lambda e: e.tensor_copy(out=self.ident_bf[:], in_=self.ident[:]), reads=[self.b_const], writes=[self.b_const])
            P.dma("sp", self.cT[:], self.cT_in[:, :, :], writes=[self.b_const])
            P.dma("sp", self.g_sb[:], self.norm_g[:, :], writes=[self.b_const])
            P.dma("sp", self.adab_sb[:], self.ada_b.rearrange("l p n -> p l n"), writes=[self.b_const])
            for pi, ph in enumerate(self.plan):
                P.barrier()
                with ExitStack() as pes:
                    self.sb = lambda name, shape, dt=F32, pi=pi, pes=pes: pes.enter_context(nc.sbuf_tensor(f"s{pi}_{name}", list(shape), dt))
                    self.ps = lambda name, shape, dt=F32, pi=pi, pes=pes: pes.enter_context(nc.psum_tensor(f"p{pi}_{name}", list(shape), dt))
                    kind = ph[0]
                    if kind == "mods":
                        self.phase_mods(ph[1])
                    elif kind == "ffn":
                        self.phase_ffn(*ph[1:])
                    elif kind == "even":
                        self.phase_even(*ph[1:])
                    elif kind == "mla":
                        self.phase_mla(*ph[1:])
                    else:
                        raise ValueError(kind)
            P.barrier()
            P.run()
        return nc

    def phase_mods(self, layers):
        nc, P = self.nc, self.P
        wada = [self.sb(f"wada{b}", [128, NCH, D], BF16) for b in range(2)]
        b_wada = [Buf(f"wada{b}") for b in range(2)]
        psmod = self.ps("psmod", [128, 2, NCH, 16], F32)
        b_ps = [Buf("psmod0"), Buf("psmod1")]
        P.op("dve", lambda e: e.memset(self.silu_c[:], 0.0), writes=[self.b_mod])
        P.op("act", lambda e: e.activation(out=self.silu_c[:, :, 0:2], in_=self.cT[:], func=AF.Silu),
             reads=[self.b_const], writes=[self.b_mod])
        it = 0
        for i in layers:
            for s in range(9):
                bsel = it % 2
                it += 1
                src = self.ada_w[i, :, s * D:(s + 1) * D].rearrange("(kc p) n -> p kc n", p=128)
                P.dma("pool", wada[bsel][:], src, writes=[b_wada[bsel]])
                for m in range(NCH):
                    for kc in range(NCH):
                        last = kc == NCH - 1
                        P.op("pe", lambda e, w=wada[bsel], m=m, kc=kc, bsel=bsel: e.matmul(
                            psmod[:, bsel, m, 0:2], w[:, kc, m * 128:(m + 1) * 128], self.silu_c[:, kc, 0:2],
                            start=(kc == 0), stop=(kc == NCH - 1)),
                            reads=[b_wada[bsel], self.b_mod], writes=[b_ps[bsel]] if (kc == 0 or last) else [],
                            inc=last)
                P.op("dve", lambda e, i=i, s=s, bsel=bsel: e.tensor_tensor(
                    out=self.modT[:, i, s * 8:(s + 1) * 8, :], in0=psmod[:, bsel, :, 0:2],
                    in1=self.adab_sb[:, i, s * 8:(s + 1) * 8].unsqueeze(2).broadcast_to([128, NCH, 2]), op=ALU.add),
                    reads=[b_ps[bsel], self.b_const], writes=[self.b_mod])
            for k in range(3):
                gk = self.g_sb[:, (i * 3 + k) * NCH:(i * 3 + k + 1) * NCH].unsqueeze(2).broadcast_to([128, NCH, 2])
                P.op("dve", lambda e, i=i, k=k, gk=gk: e.scalar_tensor_tensor(
                    out=self.modA[:, i, k, :, :], in0=self.modT[:, i, (3 * k + 1) * 8:(3 * k + 2) * 8, :],
                    scalar=1.0, in1=gk, op0=ALU.add, op1=ALU.mult),
                    reads=[self.b_mod, self.b_const], writes=[self.b_mod])
                P.op("dve", lambda e, i=i, k=k: e.tensor_scalar(
                    self.modG[:, i, k, :, :], self.modT[:, i, (3 * k + 2) * 8:(3 * k + 3) * 8, :],
                    0.5 if k != 1 else 1.0, None, ALU.mult),
                    reads=[self.b_mod], writes=[self.b_mod])

    def load_x_tile(self, ti, xt, b_xt, src, xin=None, b_xin=None, pst=None, b_pst=None, off=0, lo=0, hi=0, part="both"):
        P = self.P
        col0, n, r = self.tiles[ti]
        if src == "scratch":
            if part == "b":
                return
            xT = self.xTs[self.xcur]
            bufs = self.xT_bufs[self.xcur]
            P.dma("sp", xt[:, :, off - lo:off + n + hi], xT[:, col0 - lo:col0 + n + hi].rearrange("(c p) t -> p c t", p=128),
                  reads=[bufs[ti]] + ([bufs[ti - 1]] if lo else []) + ([bufs[ti + 1]] if hi else []), writes=[b_xt])
            return
        rows = self.x_in[col0:col0 + n, :] if r == 0 else self.ctx_in[:, :]
        nb = n // 128
        if part in ("both", "a"):
            P.dma("sp", xin[:, :nb, :], rows.rearrange("(b p) f -> p b f", p=128), writes=[b_xin])
        if part == "a":
            return
        for c in range(NCH):
            for b in range(nb):
                P.op("pe", lambda e, b=b, c=c: e.transpose(
                    pst[:, b * 128:(b + 1) * 128], xin[:, b, c * 128:(c + 1) * 128], self.ident[:]),
                    reads=[b_xin, self.b_const], writes=[b_pst], inc=(b == nb - 1))
            P.op("act", lambda e, c=c: e.copy(out=xt[:, c, off:off + n], in_=pst[:, :n]), reads=[b_pst], writes=[b_xt])

    def store_x_tile(self, ti, xt, b_xt, dst, ot=None, b_ot=None, pso=None, b_pso=None, off=0, flip=False):
        P = self.P
        col0, n, r = self.tiles[ti]
        if dst == "scratch":
            idx = (1 - self.xcur) if flip else self.xcur
            P.dma("sp", self.xTs[idx][:, col0:col0 + n].rearrange("(c p) t -> p c t", p=128), xt[:, :, off:off + n],
                  reads=[b_xt], writes=[self.xT_bufs[idx][ti]])
            return
        if r == 1 and not self.debug:
            return
        target = self.out if r == 0 else self.out_ctx
        nb = n // 128
        for b in range(nb):
            for half in range(2):
                for cc in range(4):
                    c = half * 4 + cc
                    P.op("pe", lambda e, b=b, c=c, cc=cc, half=half: e.transpose(
                        pso[half][:, cc * 128:(cc + 1) * 128], xt[:, c, off + b * 128:off + (b + 1) * 128], self.ident[:]),
                        reads=[b_xt, self.b_const], writes=[b_pso[half]], inc=(cc == 3))
                P.op("act", lambda e, b=b, half=half: e.copy(out=ot[:, b, half * 512:(half + 1) * 512], in_=pso[half][:, :]),
                     reads=[b_pso[half]], writes=[b_ot])
        c0 = col0 if r == 0 else 0
        P.dma("sp", target[c0:c0 + n, :].rearrange("(b p) f -> p b f", p=128), ot[:, :nb, :], reads=[b_ot])

    def norm_steps(self, xt, b_xt, n, i, k, r, h, b_h, W, off=0):
        P = self.P
        sq, b_sq, pss, b_pss, rt, b_rt, rstd, b_rstd, tmp, b_tmp = W
        cs = slice(off, off + n)
        steps = []
        steps.append(lambda: P.op("act", lambda e: e.activation(out=sq[:, :, cs], in_=xt[:, :, cs], func=AF.Square), reads=[b_xt], writes=[b_sq]))

        def ss():
            for c in range(NCH):
                P.op("pe", lambda e, c=c: e.matmul(pss[:, :n], self.ones_bf[:], sq[:, c, cs], start=(c == 0), stop=(c == NCH - 1)),
                     reads=[b_sq, self.b_const], writes=[b_pss] if c in (0, NCH - 1) else [], inc=(c == NCH - 1))
        steps.append(ss)

        def rs():
            P.op("act", lambda e: e.activation(out=rt[:, :n], in_=pss[:, :n], func=AF.Sqrt, bias=self.eps_col[:, 0:1], scale=1.0 / D),
                 reads=[b_pss, self.b_const], writes=[b_rt])
            P.op("dve", lambda e: e.reciprocal(out=rstd[:, :n], in_=rt[:, :n]), reads=[b_rt], writes=[b_rstd])
        steps.append(rs)
        for half in range(2):
            hs_ = slice(half * 4, half * 4 + 4)
            steps.append(lambda hs_=hs_: P.op("dve", lambda e: e.tensor_tensor(
                out=tmp[:, hs_, cs], in0=xt[:, hs_, cs], in1=rstd[:, :n].unsqueeze(1).broadcast_to([128, 4, n]), op=ALU.mult),
                reads=[b_xt, b_rstd], writes=[b_tmp]))
        for c in range(NCH):
            steps.append(lambda c=c: P.op("act", lambda e: e.activation(
                out=h[:, c, cs], in_=tmp[:, c, cs], func=AF.Identity, bias=self.modT[:, i, 3 * k * 8 + c, r:r + 1],
                scale=self.modA[:, i, k, c, r:r + 1]), reads=[b_tmp, self.b_mod], writes=[b_h]))
        return steps

    def norm_tile(self, xt, b_xt, n, i, k, r, h, b_h, W, off=0):
        for st in self.norm_steps(xt, b_xt, n, i, k, r, h, b_h, W, off):
            st()

    def norm_work(self, tag=""):
        sb, ps = self.sb, self.ps
        return (sb("sq" + tag, [128, NCH, NT], BF16), Buf("sq"), ps("pss" + tag, [128, 512], F32), Buf("pss"),
                sb("rt" + tag, [128, NT], F32), Buf("rt"), sb("rstd" + tag, [128, NT], F32), Buf("rstd"),
                sb("tmp" + tag, [128, NCH, NT], F32), Buf("tmp"))

    def phase_ffn(self, i, kf, src, dst, skip_ctx=False):
        nc, P, sb, ps = self.nc, self.P, self.sb, self.ps
        k = 0 if kf == 0 else 2
        w_in = sb("w_in", [128, NCH, 2 * DFF], BF16)
        w_out = sb("w_out", [128, NJ, D], BF16)
        jr = [(0, 6), (6, 12), (12, 17), (17, 22)]
        piece_of = {}
        for q, (j0, j1) in enumerate(jr):
            for j in range(j0, j1):
                piece_of[j] = q
        b_wa = [Buf("wa%d" % q) for q in range(4)]
        b_wb = [Buf("wb%d" % q) for q in range(4)]
        b_wo = [Buf("wo%d" % q) for q in range(4)]
        for q, (j0, j1) in enumerate(jr):
            for half, bb in ((0, b_wa), (1, b_wb)):
                cs = slice(half * DFF + j0 * 128, half * DFF + j1 * 128)
                P.dma("pool", w_in[:, :, cs], self.ffn_w_in[i, kf, :, cs].rearrange("(kc p) n -> p kc n", p=128), writes=[bb[q]])
            P.dma("pool", w_out[:, j0:j1, :], self.ffn_w_out[i, kf, j0 * 128:j1 * 128, :].rearrange("(j p) n -> p j n", p=128), writes=[b_wo[q]])
        xts = [sb(f"xt{b}", [128, NCH, NT], F32) for b in range(2)]
        b_xts = [Buf(f"xt{b}") for b in range(2)]
        W = self.norm_work()
        h = sb("h", [128, NCH, NT], BF16)
        b_h = Buf("h")
        sa = [sb(f"sa{b}", [128, NT], F32) for b in range(2)]
        b_sa = [Buf("sa") for _ in range(2)]
        u = [sb(f"u{b}", [128, NT], BF16) for b in range(3)]
        b_u = [Buf("u") for _ in range(3)]
        psab = ps("psab", [128, 2, 2, NT], F32)
        b_psa = [Buf("psa") for _ in range(2)]
        b_psb = [Buf("psb") for _ in range(2)]
        psy = ps("psy", [128, NCH, NT], F32)
        b_psy = Buf("psy")
        xin = b_xin = pst = b_pst = ot = b_ot = pso = b_pso = None
        if src == "input" or dst == "output":
            pst, b_pst = ps("pst", [128, 512], F32), Buf("pst")
        if src == "input":
            xin, b_xin = sb("xin", [128, 2, D], F32), Buf("xin")
        if dst == "output":
            ot, b_ot = sb("ot", [128, 2, D], F32), Buf("ot")
            pso, b_pso = [pst, pst], [b_pst, b_pst]

        hs = [h, sb("h1", [128, NCH, NT], BF16)]
        b_hs = [b_h, Buf("h1")]
        tl = [ti for ti in range(len(self.tiles)) if not (self.tiles[ti][2] == 1 and skip_ctx)]

        def prep_a(idx):
            ti = tl[idx]
            self.load_x_tile(ti, xts[idx % 2], b_xts[idx % 2], src, xin, b_xin, pst, b_pst, part="a")

        def prep_b(idx):
            ti = tl[idx]
            self.load_x_tile(ti, xts[idx % 2], b_xts[idx % 2], src, xin, b_xin, pst, b_pst, part="b")

        def prep_norm(idx):
            ti = tl[idx]
            col0, n, r = self.tiles[ti]
            self.norm_tile(xts[idx % 2], b_xts[idx % 2], n, i, k, r, hs[idx % 2], b_hs[idx % 2], W)

        def prep_norm_steps(idx):
            ti = tl[idx]
            col0, n, r = self.tiles[ti]
            return self.norm_steps(xts[idx % 2], b_xts[idx % 2], n, i, k, r, hs[idx % 2], b_hs[idx % 2], W)

        def do_tile(idx):
            ti = tl[idx]
            col0, n, r = self.tiles[ti]
            xt, b_xt = xts[idx % 2], b_xts[idx % 2]
            hh, b_hh = hs[idx % 2], b_hs[idx % 2]
            has_next = idx + 1 < len(tl)
            if has_next:
                prep_a(idx + 1)

            def ab(j):
                bsel = j % 2
                for half in (0, 1):
                    for kc in range(NCH):
                        cs = slice(half * DFF + j * 128, half * DFF + (j + 1) * 128)
                        P.op("pe", lambda e, kc=kc, cs=cs, half=half, bsel=bsel: e.matmul(
                            psab[:, bsel, half, :n], w_in[:, kc, cs], hh[:, kc, :n], start=(kc == 0), stop=(kc == NCH - 1)),
                            reads=[b_hh, b_wa[piece_of[j]], b_wb[piece_of[j]]], writes=[b_psa[bsel]] if kc in (0, NCH - 1) else [], inc=(kc == NCH - 1))
                P.op("act", lambda e, bsel=bsel: e.activation(out=sa[bsel][:, :n], in_=psab[:, bsel, 0, :n], func=AF.Silu),
                     reads=[b_psa[bsel]], writes=[b_sa[bsel]])
                P.op("dve", lambda e, bsel=bsel, j=j: e.tensor_tensor(out=u[j % 3][:, :n], in0=sa[bsel][:, :n],
                                                                      in1=psab[:, bsel, 1, :n], op=ALU.mult),
                     reads=[b_sa[bsel], b_psa[bsel]], writes=[b_u[j % 3]])

            def yacc(j):
                for m in range(NCH):
                    P.op("pe", lambda e, m=m, j=j: e.matmul(
                        psy[:, m, :n], w_out[:, j, m * 128:(m + 1) * 128], u[j % 3][:, :n], start=(j == 0 and m % 2 == 0),
                        stop=(j == NJ - 1), skip_group_check=True),
                        reads=[b_u[j % 3], b_wo[piece_of[j]]], writes=[b_psy] if (j in (0, NJ - 1)) else [],
                        inc=(m == NCH - 1))

            ab(0)
            nsteps = []
            for j in range(1, NJ):
                ab(j)
                yacc(j - 1)
                if has_next and j == 4:
                    prep_b(idx + 1)
                if has_next and j == 6:
                    nsteps = prep_norm_steps(idx + 1)
                if j >= 7 and nsteps:
                    nsteps.pop(0)()
            while nsteps:
                nsteps.pop(0)()
            yacc(NJ - 1)
            for m in range(NCH):
                P.op("dve", lambda e, m=m: e.scalar_tensor_tensor(
                    out=xt[:, m, :n], in0=psy[:, m, :n], scalar=self.modG[:, i, k, m, r:r + 1], in1=xt[:, m, :n],
                    op0=ALU.mult, op1=ALU.add),
                    reads=[b_psy, b_xt, self.b_mod], writes=[b_xt])
            self.store_x_tile(ti, xt, b_xt, dst, ot, b_ot, pso, b_pso)

        prep_a(0)
        prep_b(0)
        prep_norm(0)
        for idx in range(len(tl)):
            do_tile(idx)
        if self.debug:
            d_h = nc.dram_tensor(f"dbg_h{i}{kf}", [128, NCH * NT], F32, kind="ExternalOutput").ap()
            P.dma("pool", d_h[:, :], h[:].rearrange("p c t -> p (c t)"), reads=[b_h, b_hs[1]])
            d_r = nc.dram_tensor(f"dbg_r{i}{kf}", [128, NT], F32, kind="ExternalOutput").ap()
            P.dma("sp", d_r[:, :], W[6][:, :], reads=[W[7]])
            d_u = nc.dram_tensor(f"dbg_u{i}{kf}", [128, NT], F32, kind="ExternalOutput").ap()
            P.dma("pool", d_u[:, :], u[(NJ - 1) % 3][:, :], reads=[b_u[(NJ - 1) % 3]])


    def declare_mixer_inputs(self, di):
        T = self.T
        self.ab_w_in = di("ab_w_in", [2, D, 2560])
        self.ab_w_sw = di("ab_w_sw", [2, D, 768])
        self.ab_w_out = di("ab_w_out", [2, D, D])
        self.pool_w = di("pool_w", [2, 4, 64, 64])
        self.pool_scale_in = di("pool_scale", [128, 4])
        self.ret_norm_g_in = di("ret_norm_g", [128, 12])
        self.lgcol_in = di("lgcol", [128, 2, 2, 3])
        self.lgbc_in = di("lgbc", [128, 24])
        self.rtab_in = di("rtab", [8, 128, 128])
        self.rope_r_in = di("rope_r", [2, 128, T])
        self.invc_in = di("invc", [2, 128, T + CTX])
        nchk = T // 128 + 2
        self.sb_scr = self.nc.dram_tensor("sb_scr", [nchk, 128, 384], BF16, kind="Internal").ap()
        self.sb_bufs = [Buf(f"sbscr{q}") for q in range(nchk)]

    def phase_even(self, i, src="scratch", dst="scratch"):
        nc, P, sb, ps = self.nc, self.P, self.sb, self.ps
        jl = i // 2
        T = self.T
        E = 8
        NW = NT + 2 * E
        NQK = 3
        w_ab = sb("w_ab", [128, NCH, 3328], BF16)
        b_wab = Buf("w_ab")
        for q in range(2):
            cs = slice(q * 1280, (q + 1) * 1280)
            P.dma("pool", w_ab[:, :, cs], self.ab_w_in[jl, :, cs].rearrange("(kc p) n -> p kc n", p=128), writes=[b_wab])
        P.dma("pool", w_ab[:, :, 2560:3328], self.ab_w_sw[jl, :, :].rearrange("(kc p) n -> p kc n", p=128), writes=[b_wab])
        w_abo = sb("w_abo", [128, NCH, D], BF16)
        b_wabo = Buf("w_abo")
        P.dma("pool", w_abo[:, :, :], self.ab_w_out[jl, :, :].rearrange("(kc p) n -> p kc n", p=128), writes=[b_wabo])
        wblk = sb("wblk", [128, 2, 128], BF16)
        b_tab = Buf("tables")
        P.op("dve", lambda e: e.memset(wblk[:], 0.0), writes=[b_tab])
        for g in range(4):
            P.dma("pool", wblk[(g % 2) * 64:(g % 2) * 64 + 64, g // 2, (g % 2) * 64:(g % 2) * 64 + 64], self.pool_w[jl, g, :, :], writes=[b_tab])
        rtab = sb("rtab", [128, 8, 128], F32)
        P.dma("sp", rtab[:], self.rtab_in.rearrange("a p n -> p a n"), writes=[b_tab])
        lgcol = sb("lgcol", [128, 2, 3], F32)
        P.dma("sp", lgcol[:], self.lgcol_in[:, jl, :, :], writes=[b_tab])
        lgbc = sb("lgbc", [128, 12], F32)
        P.dma("sp", lgbc[:], self.lgbc_in[:, jl * 12:(jl + 1) * 12], writes=[b_tab])
        pscale = sb("pscale", [128, 2], F32)
        P.dma("sp", pscale[:], self.pool_scale_in[:, jl * 2:(jl + 1) * 2], writes=[b_tab])
        rng = sb("rng", [128, 6], F32)
        P.dma("sp", rng[:], self.ret_norm_g_in[:, jl * 6:(jl + 1) * 6], writes=[b_tab])
        gC = sb("gC", [128, 2, 3], F32)
        P.op("act", lambda e: e.activation(out=gC[:], in_=lgcol[:], func=AF.Exp, scale=128.0), reads=[b_tab], writes=[b_tab])
        DT = sb("DT", [128, 6, 128], BF16)
        dtmp = [sb(f"dtmp{q}", [128, 128], F32) for q in range(2)]
        for hh in range(6):
            P.op("act", lambda e, hh=hh: e.activation(out=dtmp[0][:], in_=rtab[:, 0, :], func=AF.Exp, scale=lgbc[:, hh:hh + 1]), reads=[b_tab], writes=[b_tab])
            P.op("act", lambda e, hh=hh: e.activation(out=dtmp[1][:], in_=rtab[:, 1, :], func=AF.Exp, scale=lgbc[:, 6 + hh:7 + hh]), reads=[b_tab], writes=[b_tab])
            P.op("dve", lambda e: e.tensor_tensor(out=dtmp[0][:], in0=dtmp[0][:], in1=rtab[:, 2, :], op=ALU.mult), reads=[b_tab], writes=[b_tab])
            P.op("dve", lambda e: e.tensor_tensor(out=dtmp[1][:], in0=dtmp[1][:], in1=rtab[:, 3, :], op=ALU.mult), reads=[b_tab], writes=[b_tab])
            P.op("dve", lambda e: e.tensor_tensor(out=dtmp[0][:], in0=dtmp[0][:], in1=dtmp[1][:], op=ALU.add), reads=[b_tab], writes=[b_tab])
            P.op("dve", lambda e, hh=hh: e.tensor_scalar(DT[:, hh, :], dtmp[0][:], 0.125, None, ALU.mult), reads=[b_tab], writes=[b_tab])
        XI = sb("XI", [128, 4, 3, 128], F32)
        for a in range(4):
            d = a % 2
            for pr in range(3):
                P.op("act", lambda e, a=a, d=d, pr=pr: e.activation(out=XI[:, a, pr, :], in_=rtab[:, 4 + a, :], func=AF.Exp, scale=lgcol[:, d, pr:pr + 1]),
                     reads=[b_tab], writes=[b_tab])
        P.op("dve", lambda e: e.tensor_scalar(XI[:, 2:4, :, :], XI[:, 2:4, :, :], 0.125, None, ALU.mult), reads=[b_tab], writes=[b_tab])
        xts = [sb(f"xt{b}", [128, NCH, NW], F32) for b in range(2)]
        b_xts = [Buf(f"xt{b}") for b in range(2)]
        Wn = (sb("sq", [128, NCH, NW], BF16), Buf("sq"), ps("pss", [128, 512], F32), Buf("pss"),
              sb("rt", [128, NW], F32), Buf("rt"), sb("rstd", [128, NW], F32), Buf("rstd"),
              sb("tmp", [128, NCH, NW], F32), Buf("tmp"))
        h = sb("h", [128, NCH, NW], BF16)
        b_h = Buf("h")
        pp = [ps(f"pp{q}", [128, 512], F32) for q in range(2)]
        b_pp = [Buf(f"pp{q}") for q in range(2)]
        ppi = [0]
        pS = [ps(f"pS{q}", [128, 512], F32) for q in range(2)]
        b_pS = [Buf(f"pS{q}") for q in range(2)]
        pR = ps("pR", [128, 512], F32)
        b_pR = Buf("pR")
        pK = ps("pK", [128, 2, 256], F32)
        b_pK = [Buf("pK")] * 2
        pT = ps("pT", [128, 3, 128], BF16)
        b_pT = Buf("pT")
        xin, b_xin = (sb("xin", [128, 2, D], F32), Buf("xin")) if src == "input" else (None, None)
        ot, b_ot = (sb("ot", [128, 2, D], F32), Buf("ot")) if dst == "output" else (None, None)
        pst, b_pst = pR, b_pR
        rope = sb("rope", [128, 2, NT], F32)
        b_rope = Buf("rope")
        kr = sb("kr", [128, NQK, NT], F32)
        b_kr = Buf("kr")
        qr = sb("qr", [128, NQK, NT], F32)
        b_qr = Buf("qr")
        t1 = sb("t1", [128, NT], F32)
        b_t1 = Buf("t1")
        k_bf = sb("k_bf", [128, NQK, NT], BF16)
        q_bf = sb("q_bf", [128, NQK, NT], BF16)
        qf_bf = sb("qf_bf", [128, NQK, NT], BF16)
        qb_bf = sb("qb_bf", [128, NQK, NT], BF16)
        b_qk = Buf("qkbf")
        khat = sb("khat", [128, NQK, 128], BF16)
        b_khat = Buf("khat")
        kt_sb = sb("kt_sb", [128, NQK, 128], BF16)
        b_kt = Buf("kt")
        v_sb = sb("v_sb", [128, 2, 768], BF16)
        b_v = Buf("v")
        S32 = [sb(f"S32_{d}", [128, 3, 128], F32) for d in range(2)]
        Sbf = sb("Sbf", [128, 3, 128], BF16)
        b_S = [Buf("Sf"), Buf("Sb")]
        b_Sbf = Buf("Sbf")
        sbt = [sb(f"sbt{q}", [128, 3, 128], BF16) for q in range(2)]
        b_sbt = [Buf("sbt0"), Buf("sbt1")]
        sbo = [sb(f"sbo{q}", [128, 3, 128], BF16) for q in range(2)]
        b_sbo = [Buf("sbo0"), Buf("sbo1")]
        mT = [sb(f"mT{q}", [128, 128], BF16) for q in range(2)]
        b_mT = [Buf("mT0"), Buf("mT1")]
        retT = sb("retT", [128, 6, NT], F32)
        b_ret = Buf("retT")
        gs = sb("gs", [128, 6, NT], F32)
        b_gs = Buf("gs")
        yT = sb("yT", [128, NCH, NT], BF16)
        b_yT = Buf("yT")
        pext = sb("pext", [128, 2, NW], F32)
        b_pext = Buf("pext")
        pa = [sb(f"pa{q}", [128, NW], F32) for q in range(3)]
        b_pa = Buf("pa")
        invc = sb("invc", [128, 2, NT], F32)
        b_invc = Buf("invc")
        rsq = sb("rsq", [128, NT], BF16)
        b_rsq = Buf("rsq")
        rr = sb("rr", [128, NT], F32)
        b_rr = Buf("rr")
        nchk_lat = T // 128

        def next_pp():
            q = ppi[0] % 2
            ppi[0] += 1
            return pp[q], b_pp[q]

        def proj_fm(col0w, width, h_lo, n):
            pt, b_pt = next_pp()
            for kc in range(NCH):
                P.op("pe", lambda e, kc=kc, pt=pt: e.matmul(pt[:width, :n], w_ab[:, kc, col0w:col0w + width], h[:, kc, h_lo:h_lo + n],
                                                           start=(kc == 0), stop=(kc == NCH - 1)),
                     reads=[b_h, b_wab], writes=[b_pt] if kc in (0, NCH - 1) else [], inc=(kc == NCH - 1))
            return pt, b_pt

        def load_tile(ti, xt, b_xt, halo):
            col0, n, r = self.tiles[ti]
            lo = E if (halo and src == "scratch" and r == 0 and col0 > 0) else 0
            hi = E if (halo and src == "scratch" and r == 0 and col0 + n < T) else 0
            self.load_x_tile(ti, xt, b_xt, src, xin, b_xin, pst, b_pst, off=E, lo=lo, hi=hi)
            return lo, hi

        def qk_rope(col_w, col_sw, dst32, b_dst, n, r):
            for c in range(NQK):
                pt, b_pt = proj_fm(col_w + c * 128, 128, E, n)
                if r == 1:
                    P.op("act", lambda e, c=c, pt=pt: e.copy(out=dst32[:, c, :n], in_=pt[:, :n]), reads=[b_pt], writes=[b_dst])
                    continue
                pt2, b_pt2 = proj_fm(col_sw + c * 128, 128, E, n)
                P.op("dve", lambda e, c=c, pt=pt: e.tensor_tensor(out=dst32[:, c, :n], in0=pt[:, :n], in1=rope[:, 0, :n], op=ALU.mult),
                     reads=[b_pt, b_rope], writes=[b_dst])
                P.op("dve", lambda e, pt2=pt2: e.tensor_tensor(out=t1[:, :n], in0=pt2[:, :n], in1=rope[:, 1, :n], op=ALU.mult),
                     reads=[b_pt2, b_rope], writes=[b_t1])
                P.op("dve", lambda e, c=c: e.tensor_tensor(out=dst32[:, c, :n], in0=dst32[:, c, :n], in1=t1[:, :n], op=ALU.add),
                     reads=[b_t1, b_dst], writes=[b_dst])

        def v_proj(n):
            for blk in range(n // 128):
                for half in range(2):
                    pt, b_pt = next_pp()
                    for kc in range(NCH):
                        P.op("pe", lambda e, kc=kc, pt=pt, blk=blk, half=half: e.matmul(
                            pt[:, :384], h[:, kc, E + blk * 128:E + (blk + 1) * 128], w_ab[:, kc, 1024 + half * 384:1024 + (half + 1) * 384],
                            start=(kc == 0), stop=(kc == NCH - 1)),
                            reads=[b_h, b_wab], writes=[b_pt] if kc in (0, NCH - 1) else [], inc=(kc == NCH - 1))
                    P.op("act", lambda e, pt=pt, blk=blk, half=half: e.copy(out=v_sb[:, blk, half * 384:(half + 1) * 384], in_=pt[:, :384]),
                         reads=[b_pt], writes=[b_v])

        def state_update(d, blk, zidx):
            P.op("dve", lambda e: e.tensor_tensor(out=khat[:, :, :], in0=kr[:, :, blk * 128:(blk + 1) * 128], in1=XI[:, zidx, :, :], op=ALU.mult),
                 reads=[b_kr, b_tab], writes=[b_khat])
            for c in range(NQK):
                P.op("pe", lambda e, c=c: e.transpose(pT[:, c, :], khat[:, c, :], self.ident_bf[:]),
                     reads=[b_khat, self.b_const], writes=[b_pT], inc=(c == NQK - 1))
            P.op("act", lambda e: e.copy(out=kt_sb[:], in_=pT[:]), reads=[b_pT], writes=[b_kt])
            for c in range(NQK):
                rg = c % 2
                P.op("pe", lambda e, c=c, rg=rg: e.matmul(pK[:, rg, :], kt_sb[:, c, :], v_sb[:, blk, c * 256:(c + 1) * 256], start=True, stop=True),
                     reads=[b_kt, b_v], writes=[b_pK[rg]])
                for hf in range(2):
                    prt = slice(hf * 64, hf * 64 + 64)
                    P.op("dve", lambda e, c=c, rg=rg, hf=hf, prt=prt: e.scalar_tensor_tensor(
                        out=S32[d][prt, c, :], in0=S32[d][prt, c, :], scalar=gC[prt, d, c:c + 1],
                        in1=pK[prt, rg, hf * 128:(hf + 1) * 128], op0=ALU.mult, op1=ALU.add),
                        reads=[b_pK[rg], b_S[d], b_tab], writes=[b_S[d]])

        P.op("dve", lambda e: e.memset(S32[1][:], 0.0), writes=[b_S[1]])
        P.op("dve", lambda e: e.memset(S32[0][:], 0.0), writes=[b_S[0]])
        ntl = len(self.tiles)
        order1 = [ntl - 1] + list(range(ntl - 2, -1, -1))
        sbo_i = [0]

        def sweep1_tile(ti, xt, b_xt):
            col0, n, r = self.tiles[ti]
            self.norm_tile(xt, b_xt, n, i, 1, r, h, b_h, Wn, off=E)
            if r == 0:
                P.dma("sp", rope[:, :, :n], self.rope_r_in[:, :, col0:col0 + n].rearrange("a p t -> p a t"), writes=[b_rope])
            qk_rope(640, 2560 + 384, kr, b_kr, n, r)
            v_proj(n)
            for blk in range(n // 128 - 1, -1, -1):
                gchunk = (col0 // 128 + blk) if r == 0 else (nchk_lat + blk)
                q = sbo_i[0] % 2
                sbo_i[0] += 1
                P.op("act", lambda e, q=q: e.copy(out=sbo[q][:], in_=S32[1][:]), reads=[b_S[1]], writes=[b_sbo[q]])
                P.dma("sp", self.sb_scr[gchunk, :, :], sbo[q][:].rearrange("p a b -> p (a b)"), reads=[b_sbo[q]], writes=[self.sb_bufs[gchunk]])
                state_update(1, blk, 3)

        load_tile(order1[0], xts[0], b_xts[0], False)
        for n_, ti in enumerate(order1):
            if n_ + 1 < len(order1):
                load_tile(order1[n_ + 1], xts[(n_ + 1) % 2], b_xts[(n_ + 1) % 2], False)
            sweep1_tile(ti, xts[n_ % 2], b_xts[n_ % 2])

        order2 = [ntl - 1] + list(range(0, ntl - 1))
        sbt_i = [0]

        def sweep2_tile(ti, xt, b_xt, lo, hi):
            col0, n, r = self.tiles[ti]
            w0 = E - lo
            wn = lo + n + hi
            self.norm_tile(xt, b_xt, wn, i, 1, r, h, b_h, Wn, off=w0)
            if r == 0:
                P.dma("sp", rope[:, :, :n], self.rope_r_in[:, :, col0:col0 + n].rearrange("a p t -> p a t"), writes=[b_rope])
            P.op("pool", lambda e: e.memset(pext[:], 0.0), writes=[b_pext])
            for c in range(2):
                pt, b_pt = proj_fm(c * 128, 128, w0, wn)
                P.op("act", lambda e, c=c, pt=pt: e.copy(out=pext[:, c, w0:w0 + wn], in_=pt[:, :wn]), reads=[b_pt], writes=[b_pext])
            P.dma("sp", invc[:, :, :n], self.invc_in[:, :, col0:col0 + n].rearrange("a p t -> p a t"), writes=[b_invc])
            for g in range(4):
                w = (2, 4, 8, 16)[g]
                c, p0 = g // 2, (g % 2) * 64
                prt = slice(p0, p0 + 64)
                cur = pext[prt, c, :]
                width = NW
                lvl = 1
                bi = 0
                while lvl * 2 < w:
                    width -= lvl
                    dstb = pa[bi % 2]
                    P.op("dve", lambda e, cur=cur, dstb=dstb, width=width, lvl=lvl, prt=prt: e.tensor_tensor(
                        out=dstb[prt, :width], in0=cur[:, 0:width], in1=cur[:, lvl:lvl + width], op=ALU.add),
                        reads=[b_pext, b_pa], writes=[b_pa])
                    cur = dstb[prt, :]
                    lvl *= 2
                    bi += 1
                hw = w // 2
                P.op("dve", lambda e, cur=cur, hw=hw, prt=prt: e.tensor_tensor(
                    out=pa[2][prt, :n], in0=cur[:, E - hw:E - hw + n], in1=cur[:, E:E + n], op=ALU.add),
                    reads=[b_pext, b_pa], writes=[b_pa])
                P.op("dve", lambda e, prt=prt, c=c: e.tensor_tensor(out=pa[2][prt, :n], in0=pa[2][prt, :n], in1=invc[prt, c, :n], op=ALU.mult),
                     reads=[b_pa, b_invc], writes=[b_pa])
                P.op("dve", lambda e, prt=prt, c=c: e.tensor_tensor(out=yT[prt, c, :n], in0=pa[2][prt, :n], in1=pext[prt, c, E:E + n], op=ALU.subtract),
                     reads=[b_pa, b_pext], writes=[b_yT])
            for c in range(2):
                pt, b_pt = next_pp()
                P.op("pe", lambda e, c=c, pt=pt: e.matmul(pt[:, :n], wblk[:, c, :], yT[:, c, :n], start=True, stop=True),
                     reads=[b_yT, b_tab], writes=[b_pt])
                P.op("dve", lambda e, c=c, pt=pt: e.tensor_scalar(yT[:, c, :n], pt[:, :n], pscale[:, c:c + 1], None, ALU.mult),
                     reads=[b_pt, b_tab], writes=[b_yT])
            qk_rope(256, 2560, qr, b_qr, n, r)
            qk_rope(640, 2560 + 384, kr, b_kr, n, r)
            v_proj(n)
            P.op("act", lambda e: e.copy(out=k_bf[:, :, :n], in_=kr[:, :, :n]), reads=[b_kr], writes=[b_qk])
            P.op("act", lambda e: e.copy(out=q_bf[:, :, :n], in_=qr[:, :, :n]), reads=[b_qr], writes=[b_qk])
            for blk in range(n // 128):
                bs = slice(blk * 128, (blk + 1) * 128)
                P.op("dve", lambda e, bs=bs: e.tensor_tensor(out=qf_bf[:, :, bs], in0=qr[:, :, bs], in1=XI[:, 0, :, :], op=ALU.mult),
                     reads=[b_qr, b_tab], writes=[b_qk])
                P.op("dve", lambda e, bs=bs: e.tensor_tensor(out=qb_bf[:, :, bs], in0=qr[:, :, bs], in1=XI[:, 1, :, :], op=ALU.mult),
                     reads=[b_qr, b_tab], writes=[b_qk])
            for c in range(6):
                pt, b_pt = proj_fm(1792 + c * 128, 128, E, n)
                P.op("act", lambda e, c=c, pt=pt: e.activation(out=gs[:, c, :n], in_=pt[:, :n], func=AF.Silu), reads=[b_pt], writes=[b_gs])
            for blk in range(n // 128):
                gchunk = (col0 // 128 + blk) if r == 0 else (nchk_lat + blk)
                P.dma("sp", sbt[blk % 2][:].rearrange("p a b -> p (a b)"), self.sb_scr[gchunk, :, :], reads=[self.sb_bufs[gchunk]], writes=[b_sbt[blk % 2]])
            for blk in range(n // 128):
                self._even_chunk(blk, col0, r, nchk_lat, sbt, b_sbt, sbt_i, Sbf, b_Sbf, S32, b_S, pS, b_pS, k_bf, q_bf, qf_bf, qb_bf, b_qk,
                                 mT, b_mT, DT, b_tab, pR, b_pR, v_sb, b_v, retT, b_ret)
                state_update(0, blk, 2)
            for hh in range(6):
                P.op("act", lambda e, hh=hh: e.activation(out=rsq[:, :n], in_=retT[:, hh, :n], func=AF.Square), reads=[b_ret], writes=[b_rsq])
                pt, b_pt = next_pp()
                P.op("pe", lambda e, pt=pt: e.matmul(pt[:, :n], self.ones_bf[:], rsq[:, :n], start=True, stop=True),
                     reads=[b_rsq, self.b_const], writes=[b_pt])
                P.op("act", lambda e, pt=pt: e.activation(out=rr[:, :n], in_=pt[:, :n], func=AF.Sqrt, bias=self.eps_col[:, 0:1], scale=1.0 / 128),
                     reads=[b_pt, self.b_const], writes=[b_rr])
                P.op("dve", lambda e: e.reciprocal(out=rr[:, :n], in_=rr[:, :n]), reads=[b_rr], writes=[b_rr])
                P.op("dve", lambda e, hh=hh: e.scalar_tensor_tensor(out=retT[:, hh, :n], in0=retT[:, hh, :n], scalar=rng[:, hh:hh + 1], in1=rr[:, :n],
                                                                   op0=ALU.mult, op1=ALU.mult), reads=[b_ret, b_rr, b_tab], writes=[b_ret])
                P.op("dve", lambda e, hh=hh: e.tensor_tensor(out=yT[:, 2 + hh, :n], in0=retT[:, hh, :n], in1=gs[:, hh, :n], op=ALU.mult),
                     reads=[b_ret, b_gs], writes=[b_yT])
            for m in range(NCH):
                pt, b_pt = next_pp()
                for c in range(NCH):
                    P.op("pe", lambda e, m=m, c=c, pt=pt: e.matmul(pt[:, :n], w_abo[:, c, m * 128:(m + 1) * 128], yT[:, c, :n], start=(c == 0), stop=(c == NCH - 1)),
                         reads=[b_yT, b_wabo], writes=[b_pt] if c in (0, NCH - 1) else [], inc=(c == NCH - 1))
                P.op("dve", lambda e, m=m, pt=pt: e.scalar_tensor_tensor(out=xt[:, m, E:E + n], in0=pt[:, :n], scalar=self.modG[:, i, 1, m, r:r + 1],
                                                                        in1=xt[:, m, E:E + n], op0=ALU.mult, op1=ALU.add),
                     reads=[b_pt, b_xt, self.b_mod], writes=[b_xt])
            self.store_x_tile(ti, xt, b_xt, dst, ot, b_ot, [pst, pst], [b_pst, b_pst], off=E, flip=True)

        nbase = len(order1)
        lh = {}
        lh[0] = load_tile(order2[0], xts[nbase % 2], b_xts[nbase % 2], True)
        for n_, ti in enumerate(order2):
            if n_ + 1 < len(order2):
                lh[n_ + 1] = load_tile(order2[n_ + 1], xts[(nbase + n_ + 1) % 2], b_xts[(nbase + n_ + 1) % 2], True)
            sweep2_tile(ti, xts[(nbase + n_) % 2], b_xts[(nbase + n_) % 2], *lh[n_])
        if dst == "scratch":
            self.xcur = 1 - self.xcur

    def _even_chunk(self, blk, col0, r, nchk_lat, sbt, b_sbt, sbt_i, Sbf, b_Sbf, S32, b_S, pS, b_pS, k_bf, q_bf, qf_bf, qb_bf, b_qk,
                    mT, b_mT, DT, b_tab, pR, b_pR, v_sb, b_v, retT, b_ret):
        P = self.P
        bs = slice(blk * 128, (blk + 1) * 128)
        q = blk % 2
        P.op("act", lambda e: e.copy(out=Sbf[:], in_=S32[0][:]), reads=[b_S[0]], writes=[b_Sbf])

        def s_mm(hh):
            c, p0 = hh // 2, (hh % 2) * 64
            P.op("pe", lambda e, hh=hh, c=c, p0=p0: e.matmul(pS[hh % 2][:, :128], k_bf[p0:p0 + 64, c, bs], q_bf[p0:p0 + 64, c, bs], start=True, stop=True),
                 reads=[b_qk], writes=[b_pS[hh % 2]])

        s_mm(0)
        for hh in range(6):
            if hh + 1 < 6:
                s_mm(hh + 1)
            c, p0 = hh // 2, (hh % 2) * 64
            P.op("dve", lambda e, hh=hh: e.tensor_tensor(out=mT[hh % 2][:], in0=pS[hh % 2][:, :128], in1=DT[:, hh, :], op=ALU.mult),
                 reads=[b_pS[hh % 2], b_tab], writes=[b_mT[hh % 2]])
            P.op("pe", lambda e, hh=hh: e.matmul(pR[:, :128], v_sb[:, blk, hh * 128:(hh + 1) * 128], mT[hh % 2][:], start=True, stop=False),
                 reads=[b_v, b_mT[hh % 2]], writes=[b_pR], inc=False)
            P.op("pe", lambda e, hh=hh, c=c, p0=p0: e.matmul(pR[:, :128], Sbf[p0:p0 + 64, c, :], qf_bf[p0:p0 + 64, c, bs], start=False, stop=False),
                 reads=[b_Sbf, b_qk], writes=[], inc=False)
            P.op("pe", lambda e, hh=hh, c=c, p0=p0, q=q: e.matmul(pR[:, :128], sbt[q][p0:p0 + 64, c, :], qb_bf[p0:p0 + 64, c, bs], start=False, stop=True),
                 reads=[b_sbt[q], b_qk], writes=[b_pR])
            P.op("act", lambda e, hh=hh: e.copy(out=retT[:, hh, bs], in_=pR[:, :128]), reads=[b_pR], writes=[b_ret])


    def declare_mla_inputs(self, di):
        T, TT = self.T, self.TT
        nc = self.nc
        self.mla_w_in = di("mla_w_in", [2, D, 672])
        self.mla_w_krsw = di("mla_w_krsw", [2, D, 96])
        self.mla_w_qb = di("mla_w_qb", [2, 384, 768])
        self.mla_w_qbsw = di("mla_w_qbsw", [2, 384, 768])
        self.mla_w_kvb = di("mla_w_kvb", [2, 256, 1536])
        self.mla_w_out = di("mla_w_out", [2, D, D])
        self.mla_g_in = di("mla_g", [128, 2, 9])
        self.rope_m_in = di("rope_m", [2, 96, T])
        self.qT_scr = nc.dram_tensor("qT_scr", [8, 96, TT], BF16, kind="Internal").ap()
        self.kT_scr = nc.dram_tensor("kT_scr", [8, 96, TT], BF16, kind="Internal").ap()
        self.v_scr = nc.dram_tensor("v_scr", [TT, 1024], BF16, kind="Internal").ap()
        self.oT_scr = nc.dram_tensor("oT_scr", [8, 128, TT], BF16, kind="Internal").ap()
        self.b_qscr, self.b_kscr, self.b_vscr, self.b_oscr = Buf("qscr"), Buf("kscr"), Buf("vscr"), Buf("oscr")

    def phase_mla(self, sub, i, src="scratch", dst="scratch"):
        if sub == 1:
            self.phase_mla1(i, src)
        elif sub == 2:
            self.phase_mla2(i)
        else:
            self.phase_mla3(i, src, dst)

    def phase_mla1(self, i, src):
        nc, P, sb, ps = self.nc, self.P, self.sb, self.ps
        jl = i // 2
        last = i == DEPTH - 1
        T = self.T
        w_mi = sb("w_mi", [128, NCH, 672 + 96], BF16)
        b_w = Buf("w")
        P.dma("pool", w_mi[:, :, 0:672], self.mla_w_in[jl].rearrange("(kc p) n -> p kc n", p=128), writes=[b_w])
        P.dma("pool", w_mi[:, :, 672:768], self.mla_w_krsw[jl].rearrange("(kc p) n -> p kc n", p=128), writes=[b_w])
        w_qb = sb("w_qb", [128, 3, 2, 768], BF16)
        P.dma("pool", w_qb[:, :, 0, :], self.mla_w_qb[jl].rearrange("(kc p) n -> p kc n", p=128), writes=[b_w])
        P.dma("pool", w_qb[:, :, 1, :], self.mla_w_qbsw[jl].rearrange("(kc p) n -> p kc n", p=128), writes=[b_w])
        w_kvb = sb("w_kvb", [128, 2, 8, 192], BF16)
        P.dma("pool", w_kvb[:].rearrange("p a b c -> p a (b c)"), self.mla_w_kvb[jl].rearrange("(kc p) n -> p kc n", p=128), writes=[b_w])
        gm = sb("gm", [128, 9], F32)
        P.dma("sp", gm[:], self.mla_g_in[:, jl, :], writes=[b_w])
        xts = [sb(f"xt{q}", [128, NCH, NT], F32) for q in range(2)]
        b_xts = [Buf("xt0"), Buf("xt1")]
        Wn = self.norm_work()
        sq, b_sq, pss, b_pss, rt, b_rt, rstd, b_rstd = Wn[:8]
        h, b_h = sb("h", [128, NCH, NT], BF16), Buf("h")
        ppA, b_ppA = ps("ppA", [128, 512], F32), Buf("ppA")
        pq4, b_pq4 = ps("pq4", [128, 4, NT], F32), Buf("pq4")
        pqs4, b_pqs4 = ps("pqs4", [128, 4, NT], F32), Buf("pqs4")
        pn4, b_pn4 = ps("pn4", [128, 4, NT], F32), Buf("pn4")
        pst, b_pst = ppA, b_ppA
        xin, b_xin = (sb("xin", [128, 2, D], F32), Buf("xin")) if src == "input" else (None, None)
        qa32, b_qa32 = sb("qa32", [128, 5, NT], F32), Buf("qa32")
        qan, b_qan = sb("qan", [128, 5, NT], BF16), Buf("qan")
        rope, b_rope = sb("rope", [96, 2, NT], F32), Buf("rope")
        krr, b_krr = sb("krr", [96, NT], F32), Buf("krr")
        krsq, b_krsq = sb("krsq", [96, NT], BF16), Buf("krsq")
        krss, b_krss = sb("krss", [96, NT], F32), Buf("krss")
        t2, b_t2 = sb("t2", [96, 4, NT], F32), Buf("t2")
        t3, b_t3 = sb("t3", [96, 4, NT], F32), Buf("t3")
        s96, b_s96 = sb("s96", [96, 4, NT], BF16), Buf("s96")
        rq, b_rq = sb("rq", [96, 4, NT], F32), Buf("rq")
        qo = [sb(f"qo{q}", [96, 4, NT], BF16) for q in range(2)]
        b_qo = [Buf("qo0"), Buf("qo1")]
        ko = [sb(f"ko{q}", [96, 4, NT], BF16) for q in range(2)]
        b_ko = [Buf("ko0"), Buf("ko1")]
        vt = [sb(f"vt{q}", [128, 1024], BF16) for q in range(2)]
        b_vt = [Buf("vt0"), Buf("vt1")]
        tl = slice(64, 96)

        def group_norm(nchunks, c0, gcol0, dim, n):
            P.op("act", lambda e: e.activation(out=sq[:, c0:c0 + nchunks, :n], in_=qa32[:, c0:c0 + nchunks, :n], func=AF.Square),
                 reads=[b_qa32], writes=[b_sq])
            for c in range(nchunks):
                P.op("pe", lambda e, c=c: e.matmul(pss[:, :n], self.ones_bf[:], sq[:, c0 + c, :n], start=(c == 0), stop=(c == nchunks - 1)),
                     reads=[b_sq, self.b_const], writes=[b_pss] if c in (0, nchunks - 1) else [], inc=(c == nchunks - 1))
            P.op("act", lambda e: e.activation(out=rt[:, :n], in_=pss[:, :n], func=AF.Sqrt, bias=self.eps_col[:, 0:1], scale=1.0 / dim),
                 reads=[b_pss, self.b_const], writes=[b_rt])
            P.op("dve", lambda e: e.reciprocal(out=rstd[:, :n], in_=rt[:, :n]), reads=[b_rt], writes=[b_rstd])
            for c in range(nchunks):
                P.op("dve", lambda e, c=c: e.scalar_tensor_tensor(out=qan[:, c0 + c, :n], in0=qa32[:, c0 + c, :n], scalar=gm[:, gcol0 + c:gcol0 + c + 1],
                                                                 in1=rstd[:, :n], op0=ALU.mult, op1=ALU.mult),
                     reads=[b_qa32, b_rstd, b_w], writes=[b_qan])

        def bc4(ap2d, rows, n):
            return ap2d.unsqueeze(1).broadcast_to([rows, 4, n])

        qans = [qan, sb("qan1", [128, 5, NT], BF16)]
        b_qans = [b_qan, Buf("qan1")]
        ropes = [rope, sb("rope1", [96, 2, NT], F32)]
        b_ropes = [b_rope, Buf("rope1")]
        krrs = [krr, sb("krr1", [96, NT], F32)]
        b_krrs = [b_krr, Buf("krr1")]
        krsss = [krss, sb("krss1", [96, NT], F32)]
        b_krsss = [b_krss, Buf("krss1")]
        t2a, b_t2a = sb("t2a", [96, NT], F32), Buf("t2a")

        def stageA(ti, slot, xt, b_xt):
            col0, n, r = self.tiles[ti]
            qan_, b_qan_ = qans[slot], b_qans[slot]
            rope_, b_rope_ = ropes[slot], b_ropes[slot]
            krr_, b_krr_ = krrs[slot], b_krrs[slot]
            krss_, b_krss_ = krsss[slot], b_krsss[slot]
            steps = list(self.norm_steps(xt, b_xt, n, i, 1, r, h, b_h, Wn))
            if r == 0:
                steps.append(lambda: P.dma("sp", rope_[:, :, :n], self.rope_m_in[:, :, col0:col0 + n].rearrange("a p t -> p a t"), writes=[b_rope_]))

            def proj(c):
                for kc in range(NCH):
                    P.op("pe", lambda e, kc=kc, c=c: e.matmul(ppA[:, :n], w_mi[:, kc, c * 128:(c + 1) * 128], h[:, kc, :n],
                                                             start=(kc == 0), stop=(kc == NCH - 1)),
                         reads=[b_h, b_w], writes=[b_ppA] if kc in (0, NCH - 1) else [], inc=(kc == NCH - 1))
                P.op("act", lambda e, c=c: e.copy(out=qa32[:, c, :n], in_=ppA[:, :n]), reads=[b_ppA], writes=[b_qa32])
            for c in range(5):
                steps.append(lambda c=c: proj(c))

            def gnorm(nchunks, c0, gcol0, dim):
                P.op("act", lambda e: e.activation(out=sq[:, c0:c0 + nchunks, :n], in_=qa32[:, c0:c0 + nchunks, :n], func=AF.Square),
                     reads=[b_qa32], writes=[b_sq])
                for c in range(nchunks):
                    P.op("pe", lambda e, c=c: e.matmul(pss[:, :n], self.ones_bf[:], sq[:, c0 + c, :n], start=(c == 0), stop=(c == nchunks - 1)),
                         reads=[b_sq, self.b_const], writes=[b_pss] if c in (0, nchunks - 1) else [], inc=(c == nchunks - 1))
                P.op("act", lambda e: e.activation(out=rt[:, :n], in_=pss[:, :n], func=AF.Sqrt, bias=self.eps_col[:, 0:1], scale=1.0 / dim),
                     reads=[b_pss, self.b_const], writes=[b_rt])
                P.op("dve", lambda e: e.reciprocal(out=rstd[:, :n], in_=rt[:, :n]), reads=[b_rt], writes=[b_rstd])
                for c in range(nchunks):
                    P.op("dve", lambda e, c=c: e.scalar_tensor_tensor(out=qan_[:, c0 + c, :n], in0=qa32[:, c0 + c, :n], scalar=gm[:, gcol0 + c:gcol0 + c + 1],
                                                                     in1=rstd[:, :n], op0=ALU.mult, op1=ALU.mult),
                         reads=[b_qa32, b_rstd, b_w], writes=[b_qan_])
            steps.append(lambda: gnorm(3, 0, 0, 384))
            steps.append(lambda: gnorm(2, 3, 3, 256))

            def kr1():
                for kc in range(NCH):
                    P.op("pe", lambda e, kc=kc: e.matmul(ppA[:96, :n], w_mi[:, kc, 576:672], h[:, kc, :n], start=(kc == 0), stop=(kc == NCH - 1)),
                         reads=[b_h, b_w], writes=[b_ppA] if kc in (0, NCH - 1) else [], inc=(kc == NCH - 1))
                P.op("act", lambda e: e.activation(out=krsq[tl, :n], in_=ppA[tl, :n], func=AF.Square), reads=[b_ppA], writes=[b_krsq])
                if r == 0:
                    P.op("dve", lambda e: e.scalar_tensor_tensor(out=krr_[tl, :n], in0=ppA[tl, :n], scalar=gm[tl, 7:8], in1=rope_[tl, 0, :n], op0=ALU.mult, op1=ALU.mult),
                         reads=[b_ppA, b_rope_, b_w], writes=[b_krr_])
                else:
                    P.op("dve", lambda e: e.tensor_scalar(krr_[tl, :n], ppA[tl, :n], gm[tl, 7:8], None, ALU.mult), reads=[b_ppA, b_w], writes=[b_krr_])

            def kr2():
                if r == 0:
                    for kc in range(NCH):
                        P.op("pe", lambda e, kc=kc: e.matmul(ppA[:96, :n], w_mi[:, kc, 672:768], h[:, kc, :n], start=(kc == 0), stop=(kc == NCH - 1)),
                             reads=[b_h, b_w], writes=[b_ppA] if kc in (0, NCH - 1) else [], inc=(kc == NCH - 1))
                    P.op("dve", lambda e: e.scalar_tensor_tensor(out=t2a[tl, :n], in0=ppA[tl, :n], scalar=gm[tl, 8:9], in1=rope_[tl, 1, :n], op0=ALU.mult, op1=ALU.mult),
                         reads=[b_ppA, b_rope_, b_w], writes=[b_t2a])
                    P.op("dve", lambda e: e.tensor_tensor(out=krr_[tl, :n], in0=krr_[tl, :n], in1=t2a[tl, :n], op=ALU.add), reads=[b_t2a, b_krr_], writes=[b_krr_])
                P.op("pe", lambda e: e.matmul(ppA[:96, :n], self.ones_bf[64:96, 0:96], krsq[64:96, :n], start=True, stop=True),
                     reads=[b_krsq, self.b_const], writes=[b_ppA])
                P.op("act", lambda e: e.copy(out=krss_[:, :n], in_=ppA[:96, :n]), reads=[b_ppA], writes=[b_krss_])
            steps.append(kr1)
            steps.append(kr2)
            return steps

        def stageB(ti, slot):
            col0, n, r = self.tiles[ti]
            qan_, b_qan_ = qans[slot], b_qans[slot]
            rope_, b_rope_ = ropes[slot], b_ropes[slot]
            krr_, b_krr_ = krrs[slot], b_krrs[slot]
            krss_, b_krss_ = krsss[slot], b_krsss[slot]
            need_q = not (last and r == 1)
            steps = []

            def q1(g4):
                for a in range(4):
                    hh = 4 * g4 + a
                    for kc in range(3):
                        P.op("pe", lambda e, kc=kc, hh=hh, a=a: e.matmul(pq4[:96, a, :n], w_qb[:, kc, 0, hh * 96:(hh + 1) * 96], qan_[:, kc, :n], start=(kc == 0), stop=(kc == 2)),
                             reads=[b_qan_, b_w], writes=[b_pq4] if kc in (0, 2) else [], inc=(kc == 2))
                P.op("act", lambda e: e.activation(out=s96[:, :, :n], in_=pq4[:96, :, :n], func=AF.Square), reads=[b_pq4], writes=[b_s96])
                if r == 0:
                    for a in range(4):
                        hh = 4 * g4 + a
                        for kc in range(3):
                            P.op("pe", lambda e, kc=kc, hh=hh, a=a: e.matmul(pqs4[:96, a, :n], w_qb[:, kc, 1, hh * 96:(hh + 1) * 96], qan_[:, kc, :n], start=(kc == 0), stop=(kc == 2)),
                                 reads=[b_qan_, b_w], writes=[b_pqs4] if kc in (0, 2) else [], inc=(kc == 2))

            def q2(g4):
                for a in range(4):
                    P.op("pe", lambda e, a=a: e.matmul(pn4[:96, a, :n], self.ones_bf[0:96, 0:96], s96[:, a, :n], start=True, stop=True),
                         reads=[b_s96, self.b_const], writes=[b_pn4], inc=(a == 3))
                P.op("act", lambda e: e.activation(out=rq[:, :, :n], in_=pn4[:96, :, :n], func=AF.Sqrt, bias=self.eps_col[0:96, 0:1], scale=1.0 / 96),
                     reads=[b_pn4, self.b_const], writes=[b_rq])
                P.op("dve", lambda e: e.reciprocal(out=rq[:, :, :n], in_=rq[:, :, :n]), reads=[b_rq], writes=[b_rq])

            def q3(g4):
                qb_, bq_ = qo[g4 % 2], b_qo[g4 % 2]
                if r == 0:
                    P.op("dve", lambda e: e.scalar_tensor_tensor(out=qb_[0:64, :, :n], in0=pq4[0:64, :, :n], scalar=gm[0:64, 5:6], in1=rq[0:64, :, :n], op0=ALU.mult, op1=ALU.mult),
                         reads=[b_pq4, b_rq, b_w], writes=[bq_])
                    P.op("dve", lambda e: e.scalar_tensor_tensor(out=t3[tl, :, :n], in0=pq4[tl, :, :n], scalar=gm[tl, 5:6], in1=bc4(rope_[tl, 0, :n], 32, n), op0=ALU.mult, op1=ALU.mult),
                         reads=[b_pq4, b_rope_, b_w], writes=[b_t3])
                    P.op("dve", lambda e: e.scalar_tensor_tensor(out=t2[tl, :, :n], in0=pqs4[tl, :, :n], scalar=gm[tl, 6:7], in1=bc4(rope_[tl, 1, :n], 32, n), op0=ALU.mult, op1=ALU.mult),
                         reads=[b_pqs4, b_rope_, b_w], writes=[b_t2])
                    P.op("pool", lambda e: e.tensor_tensor(out=t3[tl, :, :n], in0=t3[tl, :, :n], in1=t2[tl, :, :n], op=ALU.add), reads=[b_t2, b_t3], writes=[b_t3])
                    P.op("pool", lambda e: e.tensor_tensor(out=qb_[tl, :, :n], in0=t3[tl, :, :n], in1=rq[tl, :, :n], op=ALU.mult), reads=[b_t3, b_rq], writes=[bq_])
                else:
                    P.op("dve", lambda e: e.scalar_tensor_tensor(out=qb_[:, :, :n], in0=pq4[:96, :, :n], scalar=gm[0:96, 5:6], in1=rq[:, :, :n], op0=ALU.mult, op1=ALU.mult),
                         reads=[b_pq4, b_rq, b_w], writes=[bq_])
                P.dma("sp", self.qT_scr[4 * g4:4 * g4 + 4, :, col0:col0 + n].rearrange("h p t -> p h t"), qb_[:, :, :n], reads=[bq_], writes=[self.b_qscr])

            def k1(g4):
                for a in range(4):
                    hh = 4 * g4 + a
                    for kc in range(2):
                        P.op("pe", lambda e, kc=kc, hh=hh, a=a: e.matmul(pq4[:64, a, :n], w_kvb[:, kc, hh, 0:64], qan_[:, 3 + kc, :n], start=(kc == 0), stop=(kc == 1)),
                             reads=[b_qan_, b_w], writes=[b_pq4], inc=(kc == 1))
                P.op("act", lambda e: e.activation(out=s96[0:64, :, :n], in_=pq4[:64, :, :n], func=AF.Square), reads=[b_pq4], writes=[b_s96])

            def k2(g4):
                for a in range(4):
                    P.op("pe", lambda e, a=a: e.matmul(pn4[:96, a, :n], self.ones_bf[0:64, 0:96], s96[0:64, a, :n], start=True, stop=True),
                         reads=[b_s96, self.b_const], writes=[b_pn4], inc=(a == 3))
                P.op("dve", lambda e: e.tensor_tensor(out=rq[:, :, :n], in0=pn4[:96, :, :n], in1=bc4(krss_[:, :n], 96, n), op=ALU.add),
                     reads=[b_pn4, b_krss_], writes=[b_rq])
                P.op("act", lambda e: e.activation(out=rq[:, :, :n], in_=rq[:, :, :n], func=AF.Sqrt, bias=self.eps_col[0:96, 0:1], scale=1.0 / 96),
                     reads=[b_rq, self.b_const], writes=[b_rq])
                P.op("dve", lambda e: e.reciprocal(out=rq[:, :, :n], in_=rq[:, :, :n]), reads=[b_rq], writes=[b_rq])

            def k3(g4):
                kb_, bk_ = ko[g4 % 2], b_ko[g4 % 2]
                P.op("dve", lambda e: e.scalar_tensor_tensor(out=kb_[0:64, :, :n], in0=pq4[:64, :, :n], scalar=gm[0:64, 7:8], in1=rq[0:64, :, :n], op0=ALU.mult, op1=ALU.mult),
                     reads=[b_pq4, b_rq, b_w], writes=[bk_])
                P.op("dve", lambda e: e.tensor_tensor(out=kb_[tl, :, :n], in0=rq[tl, :, :n], in1=bc4(krr_[tl, :n], 32, n), op=ALU.mult), reads=[b_krr_, b_rq], writes=[bk_])
                P.dma("sp", self.kT_scr[4 * g4:4 * g4 + 4, :, col0:col0 + n].rearrange("h p t -> p h t"), kb_[:, :, :n], reads=[bk_], writes=[self.b_kscr])

            def vblk(blk):
                vb_, bv_ = vt[blk % 2], b_vt[blk % 2]
                for half in range(2):
                    pv, b_pv = (pqs4, b_pqs4) if half == 0 else (pn4, b_pn4)
                    for kc in range(2):
                        P.op("pe", lambda e, kc=kc, half=half, pv=pv: e.matmul(
                            pv[:, 0:2, :].rearrange("p a (b c) -> p (a b) c", b=2), qan_[:, 3 + kc, blk * 128:(blk + 1) * 128],
                            w_kvb[:, kc, 4 * half:4 * half + 4, 64:192], start=(kc == 0), stop=(kc == 1)),
                            reads=[b_qan_, b_w], writes=[b_pv], inc=(kc == 1))
                    P.op("act", lambda e, half=half, pv=pv: e.copy(out=vb_[:, half * 512:(half + 1) * 512], in_=pv[:, 0:2, :].rearrange("p a b -> p (a b)")),
                         reads=[b_pv], writes=[bv_])
                P.dma("sp", self.v_scr[col0 + blk * 128:col0 + (blk + 1) * 128, :], vb_[:, :], reads=[bv_], writes=[self.b_vscr])

            for g4 in range(2):
                if need_q:
                    steps += [lambda g4=g4: q1(g4), lambda g4=g4: q2(g4), lambda g4=g4: q3(g4)]
                steps += [lambda g4=g4: k1(g4), lambda g4=g4: k2(g4), lambda g4=g4: k3(g4)]
            for blk in range(n // 128):
                steps.append(lambda blk=blk: vblk(blk))
            return steps

        nt_ = len(self.tiles)
        self.load_x_tile(0, xts[0], b_xts[0], src, xin, b_xin, pst, b_pst)
        if nt_ > 1:
            self.load_x_tile(1, xts[1], b_xts[1], src, xin, b_xin, pst, b_pst)
        for st in stageA(0, 0, xts[0], b_xts[0]):
            st()
        for ti in range(nt_):
            sB = stageB(ti, ti % 2)
            sA = stageA(ti + 1, (ti + 1) % 2, xts[(ti + 1) % 2], b_xts[(ti + 1) % 2]) if ti + 1 < nt_ else []
            ia = ib = 0
            while ia < len(sA) or ib < len(sB):
                if ib < len(sB):
                    sB[ib]()
                    ib += 1
                for _ in range(2):
                    if ia < len(sA):
                        sA[ia]()
                        ia += 1
            if ti + 2 < nt_:
                self.load_x_tile(ti + 2, xts[ti % 2], b_xts[ti % 2], src, xin, b_xin, pst, b_pst)

    def phase_mla2(self, i):
        nc, P, sb, ps = self.nc, self.P, self.sb, self.ps
        last = i == DEPTH - 1
        T, TT = self.T, self.TT
        NKT = TT // 128
        Kh = [sb(f"Kh{q}", [96, TT], BF16) for q in range(2)]
        Qh = [sb(f"Qh{q}", [96, TT], BF16) for q in range(2)]
        Vh = [sb(f"Vh{q}", [128, NKT, 128], BF16) for q in range(2)]
        b_K, b_Q, b_V = [Buf("K0"), Buf("K1")], [Buf("Q0"), Buf("Q1")], [Buf("V0"), Buf("V1")]
        pS = [ps(f"pS{q}", [128, 2, 512], F32) for q in range(2)]
        b_pS = [Buf(f"pS{q}") for q in range(2)]
        po = [ps(f"po{q}", [128, 512], F32) for q in range(2)]
        b_po = [Buf("po0"), Buf("po1")]
        pd = [ps(f"pd{q}", [128, 512], F32) for q in range(2)]
        b_pd = [Buf("pd0"), Buf("pd1")]
        Pt = [sb(f"Pt{q}", [128, 2, 512], BF16) for q in range(3)]
        b_Pt = [Buf(f"Pt{q}") for q in range(3)]
        accd = [sb(f"accd{q}", [128, 512], F32) for q in range(2)]
        b_accd = [Buf("accd0"), Buf("accd1")]
        rd = [sb(f"rd{q}", [128, 512], F32) for q in range(2)]
        b_rd = [Buf("rd0"), Buf("rd1")]
        osb = [sb(f"osb{q}", [128, 512], BF16) for q in range(2)]
        b_osb = [Buf("osb0"), Buf("osb1")]
        ones32 = sb("ones32", [128, 128], F32)
        b_o32 = Buf("ones32")
        P.op("dve", lambda e: e.memset(ones32[:], 1.0), writes=[b_o32])
        scale = float(96 ** -0.5)
        NQ = min(512, T)
        qtiles = [(q0, NQ, list(range(NKT))) for q0 in range(0, T, NQ)]
        if not last:
            qtiles.append((T, CTX, list(range(T // 128, NKT))))
        cnt = [0]

        def attend(hh, hb, q0, nq, kts, qi):
            o_, d_ = po[qi % 2], pd[qi % 2]
            npair = len(kts) // 2
            assert len(kts) % 2 == 0
            ad, b_ad = accd[qi % 2], b_accd[qi % 2]

            def s_mm(idx):
                g = cnt[0] + idx
                for w in range(2):
                    kt = kts[2 * idx + w]
                    P.op("pe", lambda e, kt=kt, g=g, w=w: e.matmul(pS[g % 2][:, w, :nq], Kh[hb][:, kt * 128:(kt + 1) * 128], Qh[hb][:, q0:q0 + nq], start=True, stop=True),
                         reads=[b_K[hb], b_Q[hb]], writes=[b_pS[g % 2]], inc=(w == 1))

            s_mm(0)
            for idx in range(npair):
                g = cnt[0] + idx
                if idx + 1 < npair:
                    s_mm(idx + 1)
                P.op("act", lambda e, g=g: e.activation(out=Pt[g % 3][:, :, :nq], in_=pS[g % 2][:, :, :nq], func=AF.Exp, scale=scale),
                     reads=[b_pS[g % 2]], writes=[b_Pt[g % 3]])
                for w in range(2):
                    kt = kts[2 * idx + w]
                    first, lst = (idx == 0 and w == 0), (idx == npair - 1 and w == 1)
                    P.op("pe", lambda e, kt=kt, g=g, w=w, first=first, lst=lst: e.matmul(o_[:, :nq], Vh[hb][:, kt, :], Pt[g % 3][:, w, :nq], start=first, stop=lst),
                         reads=[b_V[hb], b_Pt[g % 3]], writes=[b_po[qi % 2]] if (first or lst) else [], inc=(w == 1))
                if idx % 4 != 3:
                    P.op("pe", lambda e, g=g, idx=idx: e.matmul(d_[:, :nq], self.ones_bf[:], Pt[g % 3][:, 0, :nq], start=(idx == 0), stop=False),
                         reads=[b_Pt[g % 3], self.b_const], writes=[b_pd[qi % 2]] if idx == 0 else [])
                else:
                    P.op("dve", lambda e, g=g: e.tensor_tensor(out=ad[:, :nq], in0=ad[:, :nq], in1=Pt[g % 3][:, 0, :nq], op=ALU.add), reads=[b_Pt[g % 3], b_ad], writes=[b_ad])
                if idx == 0:
                    P.op("dve", lambda e, g=g: e.tensor_copy(out=ad[:, :nq], in_=Pt[g % 3][:, 1, :nq]), reads=[b_Pt[g % 3]], writes=[b_ad])
                else:
                    P.op("dve", lambda e, g=g: e.tensor_tensor(out=ad[:, :nq], in0=ad[:, :nq], in1=Pt[g % 3][:, 1, :nq], op=ALU.add), reads=[b_Pt[g % 3], b_ad], writes=[b_ad])
            cnt[0] += npair
            P.op("pe", lambda e: e.matmul(d_[:, :nq], ones32[:], ad[:, :nq], start=False, stop=True), reads=[b_ad, b_o32], writes=[b_pd[qi % 2]])
            P.op("dve", lambda e: e.reciprocal(out=rd[qi % 2][:, :nq], in_=d_[:, :nq]), reads=[b_pd[qi % 2]], writes=[b_rd[qi % 2]])
            P.op("dve", lambda e: e.tensor_tensor(out=osb[qi % 2][:, :nq], in0=o_[:, :nq], in1=rd[qi % 2][:, :nq], op=ALU.mult),
                 reads=[b_po[qi % 2], b_rd[qi % 2]], writes=[b_osb[qi % 2]])
            P.dma("sp", self.oT_scr[hh, :, q0:q0 + nq], osb[qi % 2][:, :nq], reads=[b_osb[qi % 2]], writes=[self.b_oscr])

        qi = 0
        TQ = T if last else TT

        def load_head(hh):
            hb = hh % 2
            P.dma("sp", Kh[hb][:, :], self.kT_scr[hh, :, :], reads=[self.b_kscr], writes=[b_K[hb]])
            P.dma("sp", Qh[hb][:, :TQ], self.qT_scr[hh, :, :TQ], reads=[self.b_qscr], writes=[b_Q[hb]])
            P.dma("sp", Vh[hb][:, :, :], self.v_scr[:, hh * 128:(hh + 1) * 128].rearrange("(kt p) e -> p kt e", p=128), reads=[self.b_vscr], writes=[b_V[hb]])

        load_head(0)
        for hh in range(8):
            hb = hh % 2
            if hh + 1 < 8:
                load_head(hh + 1)
            for (q0, nq, kts) in qtiles:
                attend(hh, hb, q0, nq, kts, qi)
                qi += 1

    def phase_mla3(self, i, src, dst):
        nc, P, sb, ps = self.nc, self.P, self.sb, self.ps
        jl = i // 2
        last = i == DEPTH - 1
        w_mo = sb("w_mo", [128, NCH, D], BF16)
        b_w = Buf("w")
        P.dma("pool", w_mo[:, :, :], self.mla_w_out[jl].rearrange("(kc p) n -> p kc n", p=128), writes=[b_w])
        xts = [sb(f"xt{q}", [128, NCH, NT], F32) for q in range(2)]
        b_xts = [Buf("xt0"), Buf("xt1")]
        ots = [sb(f"ot{q}", [128, 8, NT], BF16) for q in range(2)]
        b_ots = [Buf("ot0"), Buf("ot1")]
        pp = [ps(f"pp{q}", [128, 512], F32) for q in range(2)]
        b_pp = [Buf("pp0"), Buf("pp1")]
        pst, b_pst = ps("pst", [128, 512], F32), Buf("pst")
        xin, b_xin = (sb("xin", [128, 2, D], F32), Buf("xin")) if src == "input" else (None, None)
        oo, b_oo = (sb("oo", [128, 2, D], F32), Buf("oo")) if dst == "output" else (None, None)
        k = [0]

        def load3(ti):
            col0, n, r = self.tiles[ti]
            self.load_x_tile(ti, xts[ti % 2], b_xts[ti % 2], src, xin, b_xin, pst, b_pst)
            P.dma("sp", ots[ti % 2][:, :, :n], self.oT_scr[:, :, col0:col0 + n].rearrange("h p t -> p h t"), reads=[self.b_oscr], writes=[b_ots[ti % 2]])

        def do_tile(ti, xt, b_xt, ot, b_ot):
            col0, n, r = self.tiles[ti]
            for m in range(NCH):
                pt, b_pt = pp[k[0] % 2], b_pp[k[0] % 2]
                k[0] += 1
                for hh in range(8):
                    P.op("pe", lambda e, m=m, hh=hh, pt=pt: e.matmul(pt[:, :n], w_mo[:, hh, m * 128:(m + 1) * 128], ot[:, hh, :n], start=(hh == 0), stop=(hh == 7)),
                         reads=[b_ot, b_w], writes=[b_pt] if hh in (0, 7) else [], inc=(hh == 7))
                P.op("dve", lambda e, m=m, pt=pt: e.scalar_tensor_tensor(out=xt[:, m, :n], in0=pt[:, :n], scalar=self.modG[:, i, 1, m, r:r + 1], in1=xt[:, m, :n],
                                                                        op0=ALU.mult, op1=ALU.add), reads=[b_pt, b_xt, self.b_mod], writes=[b_xt])
            self.store_x_tile(ti, xt, b_xt, dst, oo, b_oo, [pst, pst], [b_pst, b_pst])

        tl3 = [ti for ti in range(len(self.tiles)) if not (last and self.tiles[ti][2] == 1)]
        load3(tl3[0])
        for q_, ti in enumerate(tl3):
            if q_ + 1 < len(tl3):
                load3(tl3[q_ + 1])
            do_tile(ti, xts[ti % 2], b_xts[ti % 2], ots[ti % 2], b_ots[ti % 2])


def full_plan():
    plan = [("mods", [0, 1, 2, 3])]
    for i in range(DEPTH):
        last = i == DEPTH - 1
        plan.append(("ffn", i, 0, "input" if i == 0 else "scratch", "scratch"))
        if i % 2 == 0:
            plan.append(("even", i, "scratch", "scratch"))
        else:
            plan.append(("mla", 1, i, "scratch"))
            plan.append(("mla", 2, i))
            plan.append(("mla", 3, i, "scratch", "scratch"))
        plan.append(("ffn", i, 1, "scratch", "output" if last else "scratch", last))
    return plan


GRID_W = 64


def rope_tables(T, rot_dim, nrows_rep):
    n_freq = rot_dim // 4
    inv = (np.float32(10000.0) ** (-np.arange(n_freq, dtype=np.float32) / np.float32(n_freq))).astype(np.float32)
    t = np.arange(T)
    row = (t // GRID_W).astype(np.float32)
    col = (t % GRID_W).astype(np.float32)
    ang = np.concatenate([row[:, None] * inv, col[:, None] * inv], axis=-1).astype(np.float32)
    cos = np.cos(ang).astype(np.float32)
    sin = np.sin(ang).astype(np.float32)
    cos_f = np.repeat(cos, 2, axis=1).T
    sin_f = np.repeat(sin, 2, axis=1).T
    sign = np.where(np.arange(rot_dim) % 2 == 0, -1.0, 1.0).astype(np.float32)[:, None]
    sin_f = sin_f * sign
    return np.ascontiguousarray(np.stack([np.tile(cos_f, (nrows_rep, 1)), np.tile(sin_f, (nrows_rep, 1))], axis=0))


def even_host_inputs(inputs, T):
    f = lambda a: np.ascontiguousarray(np.asarray(a, dtype=np.float32))
    ab_w_in = f(inputs["ab_w_in"])
    qk = ab_w_in[:, :, 256:1024]
    ab_w_sw = np.ascontiguousarray(qk.reshape(2, D, 384, 2)[..., ::-1].reshape(2, D, 768))
    ld = f(inputs["ret_log_decay"])
    lgcol = np.zeros((128, 2, 2, 3), np.float32)
    for pr in range(3):
        lgcol[:64, :, :, pr] = ld[:, :, 2 * pr][None]
        lgcol[64:, :, :, pr] = ld[:, :, 2 * pr + 1][None]
    lgbc = np.ascontiguousarray(np.broadcast_to(ld.reshape(1, 24), (128, 24)))
    jj = np.arange(128, dtype=np.float32)[:, None]
    ii = np.arange(128, dtype=np.float32)[None, :]
    one = np.ones((128, 1), np.float32)
    rtab = np.stack([np.maximum(ii - jj, 0), np.maximum(jj - ii, 0), (ii >= jj).astype(np.float32), (jj >= ii).astype(np.float32),
                     one * (ii + 1), one * (128 - ii), one * (127 - ii), one * ii], axis=0).astype(np.float32)
    invc = np.zeros((2, 128, T + CTX), np.float32)
    for g, w in enumerate((2, 4, 8, 16)):
        lo = w // 2
        hi = w - 1 - lo
        for (L, c0) in ((T, 0), (CTX, T)):
            t = np.arange(L)
            cnt = (np.minimum(t + hi + 1, L) - np.maximum(t - lo, 0)).astype(np.float32)
            invc[g // 2, (g % 2) * 64:(g % 2) * 64 + 64, c0:c0 + L] = (np.float32(1.0) / cnt)[None, :]
    return {
        "ab_w_in": ab_w_in, "ab_w_sw": ab_w_sw, "ab_w_out": f(inputs["ab_w_out"]), "pool_w": f(inputs["pool_w"]),
        "pool_scale": np.ascontiguousarray(f(inputs["pool_scale"]).reshape(2, 2, 128).transpose(2, 0, 1).reshape(128, 4)),
        "ret_norm_g": np.ascontiguousarray(f(inputs["ret_norm_g"]).reshape(2, 6, 128).transpose(2, 0, 1).reshape(128, 12)),
        "lgcol": lgcol, "lgbc": lgbc, "rtab": rtab, "rope_r": rope_tables(T, 64, 2), "invc": invc,
    }


def _swap_pairs(a):
    sh = a.shape
    return np.ascontiguousarray(a.reshape(sh[:-1] + (sh[-1] // 2, 2))[..., ::-1].reshape(sh))


def mla_host_inputs(inputs, T):
    f = lambda a: np.ascontiguousarray(np.asarray(a, dtype=np.float32))
    w_in = f(inputs["mla_w_in"])
    krsw = np.concatenate([w_in[:, :, 576:640], _swap_pairs(w_in[:, :, 640:672])], axis=-1)
    w_qb = f(inputs["mla_w_qb"])
    qbh = w_qb.reshape(2, 384, 8, 96)
    qbsw = np.concatenate([qbh[..., :64], _swap_pairs(qbh[..., 64:])], axis=-1).reshape(2, 384, 768)
    g = np.zeros((128, 2, 9), np.float32)
    qa_g, kva_g, qn_g, kn_g = f(inputs["mla_qa_g"]), f(inputs["mla_kva_g"]), f(inputs["mla_qn_g"]), f(inputs["mla_kn_g"])
    for jl in range(2):
        g[:, jl, 0:3] = qa_g[jl].reshape(3, 128).T
        g[:, jl, 3:5] = kva_g[jl].reshape(2, 128).T
        g[:96, jl, 5] = qn_g[jl]
        g[64:96, jl, 6] = _swap_pairs(qn_g[jl][64:])
        g[:96, jl, 7] = kn_g[jl]
        g[64:96, jl, 8] = _swap_pairs(kn_g[jl][64:])
    rm = np.zeros((2, 96, T), np.float32)
    rm[:, 64:96, :] = rope_tables(T, 32, 1)
    return {"mla_w_in": w_in, "mla_w_krsw": np.ascontiguousarray(krsw), "mla_w_qb": w_qb, "mla_w_qbsw": np.ascontiguousarray(qbsw),
            "mla_w_kvb": f(inputs["mla_w_kvb"]), "mla_w_out": f(inputs["mla_w_out"]), "mla_g": g, "rope_m": rm}


def make_in_maps(inputs, T, n_cores=8):
    f = lambda a: np.ascontiguousarray(np.asarray(a, dtype=np.float32))
    x, c, ctx, c_ctx = f(inputs["x"]), f(inputs["c"]), f(inputs["ctx"]), f(inputs["c_ctx"])
    B = x.shape[0]
    ada_b = f(inputs["ada_b"]).reshape(DEPTH, 9, NCH, 128).transpose(0, 3, 1, 2).reshape(DEPTH, 128, 72)
    norm_g = f(inputs["norm_g"]).reshape(DEPTH * 3, NCH, 128).transpose(2, 0, 1).reshape(128, DEPTH * 3 * NCH)
    shared = {
        "ident": np.eye(128, dtype=np.float32),
        "ada_w": f(inputs["ada_w"]), "ada_b": np.ascontiguousarray(ada_b), "norm_g": np.ascontiguousarray(norm_g),
        "ffn_w_in": f(inputs["ffn_w_in"]), "ffn_w_out": f(inputs["ffn_w_out"]),
    }
    shared.update(even_host_inputs(inputs, T))
    shared.update(mla_host_inputs(inputs, T))
    maps = []
    for core in range(n_cores):
        b = core % B
        cT = np.stack([c[b].reshape(NCH, 128).T, c_ctx.reshape(NCH, 128).T], axis=-1)
        m = dict(shared)
        m.update({"x": np.ascontiguousarray(x[b, :T]), "ctx": np.ascontiguousarray(ctx[b]), "cT": np.ascontiguousarray(cT)})
        maps.append(m)
    return maps


_NC_CACHE = {}


def run_plan(inputs, T, plan, n_cores=8):
    key = (T, repr(plan))
    if key not in _NC_CACHE:
        _NC_CACHE[key] = Builder(T, plan).build()
    nc = _NC_CACHE[key]
    maps = make_in_maps(inputs, T, n_cores)
    res = run_bass_kernel_spmd(nc, maps, core_ids=list(range(n_cores)))
    return res


N_CORES = 4


def kernel(**inputs):
    x = np.asarray(inputs["x"])
    B, T, _ = x.shape
    res = run_plan(inputs, T, full_plan(), n_cores=N_CORES)
    out = np.stack([np.asarray(res.results[b]["out"]) for b in range(B)], axis=0)
    return out.astype(np.float32)
```
